# Optimizing a Trainium2 kernel written in Bass

```python
import jax, jax.numpy as jnp
from jax import lax
import numpy as np

D_MODEL = 2048
BATCH = 4
SEQ = 4096
DEPTH = 2

A_HEADS = 16
A_KV_HEADS = 2
A_HEAD_DIM = 64
WINDOW = 128
B_HEADS = 8
B_HEAD_DIM = 128
CONV_K = 4
DN_CHUNK = 64
C_WIDTH = D_MODEL
C_GROUPS = 8
C_CHUNK = 128
C_GROUP_DIM = C_WIDTH // C_GROUPS
D_FF = -(-8 * D_MODEL // (3 * 256)) * 256
EPS = 1e-6

A_Q = A_HEADS * A_HEAD_DIM
A_KV = A_KV_HEADS * A_HEAD_DIM
B_W = B_HEADS * B_HEAD_DIM
EVEN_IN = A_Q + 2 * A_KV + 4 * B_W + 2 * B_HEADS
MIX_OUT = A_Q + B_W
EVEN_SPLITS = [int(s) for s in np.cumsum([A_Q, A_KV, A_KV, 3 * B_W, B_W, B_HEADS])]

kernel_name = "hybrid_swa_sink_gdn_gmlp_block"


def rms_norm(x, g):
    xf = x.astype(jnp.float32)
    y = xf * lax.rsqrt(jnp.mean(xf * xf, axis=-1, keepdims=True) + EPS)
    return (y * g.astype(jnp.float32)).astype(x.dtype)


def layer_norm(x, g, b):
    xf = x.astype(jnp.float32)
    mu = jnp.mean(xf, axis=-1, keepdims=True)
    var = jnp.mean(jnp.square(xf - mu), axis=-1, keepdims=True)
    y = (xf - mu) * lax.rsqrt(var + EPS)
    return (y * g.astype(jnp.float32) + b.astype(jnp.float32)).astype(x.dtype)


def l2_norm(x):
    return x * lax.rsqrt(jnp.sum(x * x, axis=-1, keepdims=True) + EPS)


def sliding_window_attention(q, k, v, sinks):
    b, t, hq, dh = q.shape
    hkv = k.shape[2]
    grp = hq // hkv
    nb = t // WINDOW
    qb = q.astype(jnp.float32).reshape(b, nb, WINDOW, hkv, grp, dh)

    def with_prev(x):
        xb = x.astype(jnp.float32).reshape(b, nb, WINDOW, hkv, dh)
        prev = jnp.pad(xb, ((0, 0), (1, 0), (0, 0), (0, 0), (0, 0)))[:, :-1]
        return jnp.concatenate([prev, xb], axis=2)

    kb, vb = with_prev(k), with_prev(v)
    s = jnp.einsum('bnqhgd,bnkhd->bnhgqk', qb, kb) * (dh ** -0.5)
    r = jnp.arange(WINDOW)[:, None]
    c = jnp.arange(2 * WINDOW)[None, :]
    rel = r + WINDOW - c
    blk = jnp.arange(nb)[:, None, None]
    valid = (rel >= 0) & (rel < WINDOW) & (blk * WINDOW - WINDOW + c >= 0)
    s = jnp.where(valid[None, :, None, None], s, -jnp.inf)
    sink = sinks.astype(jnp.float32).reshape(1, 1, hkv, grp, 1, 1)
    m = jnp.maximum(jnp.max(s, axis=-1, keepdims=True), sink)
    p = jnp.exp(s - m)
    denom = jnp.sum(p, axis=-1, keepdims=True) + jnp.exp(sink - m)
    o = jnp.einsum('bnhgqk,bnkhd->bnqhgd', p / denom, vb)
    return o.reshape(b, t, hq * dh).astype(q.dtype)


def causal_conv_silu(x, w):
    kk, ch = w.shape
    y = lax.conv_general_dilated(
        x, w[:, None, :].astype(x.dtype), window_strides=(1,),
        padding=((kk - 1, 0),), dimension_numbers=('NWC', 'WIO', 'NWC'),
        feature_group_count=ch)
    return jax.nn.silu(y)


def gated_delta_rule(q, k, v, beta, g):
    b, t, h, dk = q.shape
    dv = v.shape[-1]
    c = DN_CHUNK
    n = t // c

    def chunks(x):
        x = jnp.moveaxis(x, 2, 1)
        return x.reshape(b, h, n, c, *x.shape[3:])

    q, k, v, beta, g = (chunks(a) for a in (q, k, v, beta, g))
    gam = jnp.cumsum(g, axis=-1)
    idx = jnp.arange(c)
    incl = idx[:, None] >= idx[None, :]
    strict = idx[:, None] > idx[None, :]
    decay = jnp.exp(jnp.where(incl, gam[..., :, None] - gam[..., None, :], -jnp.inf))
    kk = jnp.einsum('bhnid,bhnjd->bhnij', k, k)
    a_mat = jnp.where(strict, beta[..., :, None] * kk * decay, 0.0) + jnp.eye(c, dtype=q.dtype)
    rhs = jnp.concatenate([v * beta[..., None], k * (beta * jnp.exp(gam))[..., None]], axis=-1)
    sol = lax.linalg.triangular_solve(a_mat, rhs, left_side=True, lower=True, unit_diagonal=True)
    u, w = sol[..., :dv], sol[..., dv:]
    qk = jnp.einsum('bhnid,bhnjd->bhnij', q, k) * decay
    q_dec = q * jnp.exp(gam)[..., None]
    k_dec = k * jnp.exp(gam[..., -1:] - gam)[..., None]
    g_last = jnp.exp(gam[..., -1])

    def step(state, xs):
        qd, kd, wc, uc, qkc, gl = xs
        v_new = uc - jnp.einsum('bhck,bhkv->bhcv', wc, state)
        o = jnp.einsum('bhck,bhkv->bhcv', qd, state) + jnp.einsum('bhij,bhjv->bhiv', qkc, v_new)
        state = state * gl[..., None, None] + jnp.einsum('bhck,bhcv->bhkv', kd, v_new)
        return state, o

    xs = tuple(jnp.moveaxis(a, 2, 0) for a in (q_dec, k_dec, w, u, qk, g_last))
    s0 = jnp.zeros((b, h, dk, dv), q.dtype)
    _, o = lax.scan(step, s0, xs)
    return jnp.transpose(o, (1, 0, 3, 2, 4)).reshape(b, t, h, dv)


def even_mixer(hn, w_in, conv_w, a_log, dt_bias, sinks, onorm, w_out):
    b, t, _ = hn.shape
    proj = hn @ w_in
    qa, ka, va, qkv_b, z, beta_raw, a_raw = jnp.split(proj, EVEN_SPLITS, axis=-1)
    out_a = sliding_window_attention(
        qa.reshape(b, t, A_HEADS, A_HEAD_DIM),
        ka.reshape(b, t, A_KV_HEADS, A_HEAD_DIM),
        va.reshape(b, t, A_KV_HEADS, A_HEAD_DIM), sinks)
    qkv_b = causal_conv_silu(qkv_b, conv_w).astype(jnp.float32)
    qb, kb, vb = jnp.split(qkv_b, 3, axis=-1)
    qb = l2_norm(qb.reshape(b, t, B_HEADS, B_HEAD_DIM)) * (B_HEAD_DIM ** -0.5)
    kb = l2_norm(kb.reshape(b, t, B_HEADS, B_HEAD_DIM))
    vb = vb.reshape(b, t, B_HEADS, B_HEAD_DIM)
    beta = jax.nn.sigmoid(beta_raw.astype(jnp.float32))
    g = -jnp.exp(a_log.astype(jnp.float32)) * jax.nn.softplus(
        a_raw.astype(jnp.float32) + dt_bias.astype(jnp.float32))
    o = gated_delta_rule(qb, kb, vb, beta, g)
    o = o * lax.rsqrt(jnp.mean(o * o, axis=-1, keepdims=True) + EPS) * onorm.astype(jnp.float32)
    o = o * jax.nn.silu(z.astype(jnp.float32).reshape(b, t, B_HEADS, B_HEAD_DIM))
    out_b = o.reshape(b, t, B_W).astype(hn.dtype)
    return jnp.concatenate([out_a, out_b], axis=-1) @ w_out


def odd_mixer(hn, w_in, ln_g, ln_b, w_s, b_s, w_out):
    b, t, _ = hn.shape
    zz = jax.nn.gelu(hn @ w_in, approximate=False)
    u, v = jnp.split(zz, 2, axis=-1)
    v = layer_norm(v, ln_g, ln_b)
    nb = t // C_CHUNK
    vb = v.reshape(b, nb, C_CHUNK, C_GROUPS, C_GROUP_DIM)
    ws = jnp.tril(w_s)
    mixed = jnp.einsum('gts,bnsgc->bntgc', ws, vb) + b_s.T[None, None, :, :, None]
    return (u * mixed.reshape(b, t, C_WIDTH)) @ w_out


def swiglu(hn, w_gate, w_up, w_down):
    return (jax.nn.silu(hn @ w_gate) * (hn @ w_up)) @ w_down


def setup_inputs(seed: int = 0) -> dict:
    key = jax.random.key(seed)
    ks = jax.random.split(key, 22)
    ne, no = (DEPTH + 1) // 2, DEPTH // 2
    f32 = jnp.float32

    def nrm(k, shape, scale):
        return jax.random.normal(k, shape, f32) * scale

    def gain(k, shape):
        return 1.0 + 0.05 * jax.random.normal(k, shape, f32)

    return {
        "x": nrm(ks[0], (BATCH, SEQ, D_MODEL), 1.0),
        "even_norm": gain(ks[1], (ne, D_MODEL)),
        "even_w_in": nrm(ks[2], (ne, D_MODEL, EVEN_IN), D_MODEL ** -0.5),
        "even_conv": nrm(ks[3], (ne, CONV_K, 3 * B_W), CONV_K ** -0.5),
        "even_a_log": jnp.log(jax.random.uniform(ks[4], (ne, B_HEADS), f32, 1.0, 16.0)),
        "even_dt_bias": nrm(ks[5], (ne, B_HEADS), 0.1),
        "even_sinks": nrm(ks[6], (ne, A_HEADS), 1.0),
        "even_onorm": gain(ks[7], (ne, B_HEAD_DIM)),
        "even_w_out": nrm(ks[8], (ne, MIX_OUT, D_MODEL), MIX_OUT ** -0.5),
        "odd_norm": gain(ks[9], (no, D_MODEL)),
        "odd_w_in": nrm(ks[10], (no, D_MODEL, 2 * C_WIDTH), D_MODEL ** -0.5),
        "odd_ln_g": gain(ks[11], (no, C_WIDTH)),
        "odd_ln_b": nrm(ks[12], (no, C_WIDTH), 0.02),
        "odd_w_s": nrm(ks[13], (no, C_GROUPS, C_CHUNK, C_CHUNK), C_CHUNK ** -0.5),
        "odd_b_s": 1.0 + nrm(ks[14], (no, C_GROUPS, C_CHUNK), 0.1),
        "odd_w_out": nrm(ks[15], (no, C_WIDTH, D_MODEL), C_WIDTH ** -0.5),
        "ffn_norm": gain(ks[16], (DEPTH, D_MODEL)),
        "ffn_w_gate": nrm(ks[17], (DEPTH, D_MODEL, D_FF), D_MODEL ** -0.5),
        "ffn_w_up": nrm(ks[18], (DEPTH, D_MODEL, D_FF), D_MODEL ** -0.5),
        "ffn_w_down": nrm(ks[19], (DEPTH, D_FF, D_MODEL), D_FF ** -0.5),
        "final_norm": gain(ks[20], (D_MODEL,)),
    }


def reference(x, even_norm, even_w_in, even_conv, even_a_log, even_dt_bias, even_sinks,
              even_onorm, even_w_out, odd_norm, odd_w_in, odd_ln_g, odd_ln_b, odd_w_s,
              odd_b_s, odd_w_out, ffn_norm, ffn_w_gate, ffn_w_up, ffn_w_down, final_norm):
    h = x
    for i in range(DEPTH):
        j = i // 2
        if i % 2 == 0:
            h = h + even_mixer(rms_norm(h, even_norm[j]), even_w_in[j], even_conv[j],
                               even_a_log[j], even_dt_bias[j], even_sinks[j],
                               even_onorm[j], even_w_out[j])
        else:
            h = h + odd_mixer(rms_norm(h, odd_norm[j]), odd_w_in[j], odd_ln_g[j],
                              odd_ln_b[j], odd_w_s[j], odd_b_s[j], odd_w_out[j])
        h = h + swiglu(rms_norm(h, ffn_norm[i]), ffn_w_gate[i], ffn_w_up[i], ffn_w_down[i])
    return rms_norm(h, final_norm)
```

```python
import numpy as np
from contextlib import ExitStack
import concourse.bass as bass
import concourse.mybir as mybir
from concourse.bass_utils import run_bass_kernel_spmd

F32 = mybir.dt.float32
BF16 = mybir.dt.bfloat16
AF = mybir.ActivationFunctionType
ALU = mybir.AluOpType
AX = mybir.AxisListType

D = 2048
DFF = 5632
NFT = 44
EPS = 1e-6
TP = 512
NH = TP // 512
NCST = 832
import os as _os
SELFSYNC_ALL = _os.environ.get('KSELFSYNC', '0') == '1'


class Buf:
    __slots__ = ("ap", "wev", "revs")

    def __init__(self, ap):
        self.ap = ap
        self.wev = None
        self.revs = {}


class KB:
    def __init__(self, stages):
        self.stages = stages
        self.nc = bass.Bass("TRN2", target_bir_lowering=False)
        self.es = ExitStack()
        nc = self.nc
        self.engs = {"pe": nc.tensor, "act": nc.scalar, "dve": nc.vector, "pool": nc.gpsimd, "sp": nc.sync}
        self.psem = {e: self.es.enter_context(nc.semaphore("p_" + e)) for e in ("pe", "act", "dve")}
        self.pcnt = {e: 0 for e in self.psem}
        self.pending = {e: ([], []) for e in self.psem}
        self.waited = {}
        self.NS = 12
        self.dq = {q: [self.es.enter_context(nc.semaphore("d_%s_%d" % (q, i))) for i in range(self.NS)] for q in ("sp", "pool")}
        self.dqn = {"sp": 0, "pool": 0}
        self.nbank = 0

    def sb(self, name, shape, dt):
        return self.es.enter_context(self.nc.sbuf_tensor("s_" + name, list(shape), dt))

    def dram(self, name, shape, dt=F32, kind="ExternalInput"):
        return self.nc.dram_tensor(name, list(shape), dt, kind=kind).ap()

    def _wait(self, eng, ev):
        if ev is None:
            return
        sem, val, src = ev
        if src == eng and eng == "pe":
            return
        key = (eng, sem.name if hasattr(sem, "name") else id(sem))
        if self.waited.get(key, 0) >= val:
            return
        self.engs[eng].wait_ge(sem, val)
        self.waited[key] = val

    def op(self, eng, fn, kw, rd=(), wr=(), mark=True, selfsync=False):
        if (selfsync or (SELFSYNC_ALL and eng in ("act", "dve"))) and self.pcnt[eng] > 0:
            self.engs[eng].wait_ge(self.psem[eng], self.pcnt[eng])
        for b in rd:
            self._wait(eng, b.wev)
        for b in wr:
            self._wait(eng, b.wev)
            for e in list(b.revs.values()):
                self._wait(eng, e)
        inst = fn(**kw)
        pr, pw = self.pending[eng]
        pr.extend(rd)
        pw.extend(wr)
        if mark:
            self.pcnt[eng] += 1
            inst.then_inc(self.psem[eng], 1)
            ev = (self.psem[eng], self.pcnt[eng], eng)
            for b in pr:
                b.revs[eng] = ev
            for b in pw:
                b.wev = ev
                b.revs = {}
            self.pending[eng] = ([], [])
        return inst

    def dma(self, q, out_ap, in_ap, rd=(), wr=()):
        for b in rd:
            self._wait(q, b.wev)
        for b in wr:
            self._wait(q, b.wev)
            for e in list(b.revs.values()):
                self._wait(q, e)
        n = self.dqn[q]
        sem = self.dq[q][n % self.NS]
        val = 16 * (n // self.NS + 1)
        if val > 16:
            self._wait(q, (sem, val - 16, None))
        inst = self.engs[q].dma_start(out=out_ap, in_=in_ap)
        inst.then_inc(sem, 16)
        ev = (sem, val, None)
        for b in rd:
            b.revs[("dma", q, n % self.NS)] = ev
        for b in wr:
            b.wev = ev
            b.revs = {}
        self.dqn[q] = n + 1
        return ev

    def bank(self):
        b = self.banks[self.nbank % 8]
        self.nbank += 1
        return b

    def mm(self, out, lhsT, rhs, start, stop, rd, wr, mark):
        return self.op("pe", self.nc.tensor.matmul, dict(out=out, lhsT=lhsT, rhs=rhs, start=start, stop=stop), rd=rd, wr=wr, mark=mark)

    def act(self, out, in_, func, rd, wr, **kw):
        return self.op("act", self.nc.scalar.activation, dict(out=out, in_=in_, func=func, **kw), rd=rd, wr=wr)

    def tt(self, out, in0, in1, op, rd, wr):
        return self.op("dve", self.nc.vector.tensor_tensor, dict(out=out, in0=in0, in1=in1, op=op), rd=rd, wr=wr)

    def ts(self, out, in0, s1, s2, op0, op1, rd, wr):
        kw = dict(out=out, in0=in0, scalar1=s1, scalar2=s2, op0=op0)
        if op1 is not None:
            kw["op1"] = op1
        return self.op("dve", self.nc.vector.tensor_scalar, kw, rd=rd, wr=wr)

    def stt(self, out, in0, scalar, in1, op0, op1, rd, wr):
        return self.op("dve", self.nc.vector.scalar_tensor_tensor, dict(out=out, in0=in0, scalar=scalar, in1=in1, op0=op0, op1=op1), rd=rd, wr=wr)

    def cp(self, eng, out, in_, rd, wr):
        if eng == "act":
            return self.act(out, in_, AF.Copy, rd, wr)
        return self.op("dve", self.nc.vector.tensor_copy, dict(out=out, in_=in_), rd=rd, wr=wr)

    def wload(self, dram_ap, n):
        A = self.W_N
        if self.wptr + n > A:
            self.wptr = 0
        s, e = self.wptr, self.wptr + n
        self.wptr = e
        deps = [b for (a, bn, b) in self.wlive if a < e and bn > s]
        self.wlive = [(a, bn, b) for (a, bn, b) in self.wlive if not (a < e and bn > s)]
        buf = Buf(self.warena[:, s:e])
        self.dma("pool", buf.ap, dram_ap, wr=deps + [buf])
        self.wlive.append((s, e, buf))
        return buf

    def build(self):
        nc = self.nc
        kb = self
        self.xo = self.dram("xo", [2048, D])
        self.xp = self.dram("xp", [2048, D])
        self.cst_d = self.dram("cst", [128, NCST])
        self.am0_d = self.dram("am0", [128, 128])
        self.norms_d = self.dram("norms", [128, 6, 16])
        self.win_d = self.dram("win", [43, 128, 2048])
        self.wba_d = self.dram("wba", [128, 256])
        self.conv_d = self.dram("conv", [128, 24, 4])
        self.hp_d = self.dram("hp", [128, 8 + 8 + 16 + 1])
        self.wout0_d = self.dram("wout0", [16, 128, 2048])
        self.oddu_d = self.dram("oddu", [16, 128, 2048])
        self.oddv_d = self.dram("oddv", [8, 128, 4096])
        self.lngb_d = self.dram("lngb", [128, 2, 16])
        self.wsT_d = self.dram("wsT", [128, 8, 128])
        self.bs_d = self.dram("bsb", [128, 8, 128])
        self.wout1_d = self.dram("wout1", [16, 128, 2048])
        self.wg_d = self.dram("wg", [2, 44, 128, 2048])
        self.wu_d = self.dram("wu", [2, 44, 128, 2048])
        self.wd_d = self.dram("wd", [2, 16, 128, 5632])
        self.out_d = self.dram("out", [2048, D], kind="ExternalOutput")

        self.hT_t = self.sb("hT", [128, 16, TP], F32)
        self.hT = [Buf(self.hT_t[:, i, :]) for i in range(16)]
        self.hn_t = self.sb("hnT", [128, 16, TP], BF16)
        self.hnT = [Buf(self.hn_t[:, i, :]) for i in range(16)]
        self.mix_t = self.sb("mixT", [128, 16, TP], BF16)
        self.mixT = [Buf(self.mix_t[:, i, :]) for i in range(16)]
        self.act_t = self.sb("actT", [128, 11, TP], BF16)
        self.actT = [Buf(self.act_t[:, i, :]) for i in range(11)]
        self.W_N = 10240
        self.warena = self.sb("warena", [128, self.W_N], BF16)
        self.wptr = 0
        self.wlive = []
        self.cst_t = self.sb("cst", [128, NCST], F32)
        self.cst = Buf(self.cst_t[:])
        self.norms_t = self.sb("norms", [128, 6, 16], F32)
        self.norms = Buf(self.norms_t[:])
        self.ones_t = self.sb("ones_bf", [128, 128], BF16)
        self.ones_bf = Buf(self.ones_t[:])
        self.onesf_t = self.sb("ones_f", [128, 128], F32)
        self.ones_f = Buf(self.onesf_t[:])
        self.eps_t = self.sb("eps", [128, 1], F32)
        self.epsb = Buf(self.eps_t[:])
        self.sq = [Buf(self.sb("sq%d" % i, [128, 512], BF16)[:]) for i in range(2)]
        self.rs = Buf(self.sb("rs", [128, 512], F32)[:])
        self.sg = [Buf(self.sb("sg%d" % i, [128, 512], F32)[:]) for i in range(2)]
        self.xstg = [Buf(self.sb("xstg%d" % i, [128, D], F32)[:]) for i in range(1)]
        self.nsg = 0
        self.vg_t = self.sb("vg", [128, 2, 2048], BF16)
        self.vg = [Buf(self.vg_t[:, i, :]) for i in range(2)]
        ps = [self.es.enter_context(nc.psum_tensor("ps%d" % i, [128, 512], F32)) for i in range(8)]
        self.banks = [Buf(p[:]) for p in ps]

        self.dma("sp", self.cst.ap, self.cst_d, wr=[self.cst])
        self.dma("sp", self.norms.ap, self.norms_d, wr=[self.norms])
        self.op("dve", nc.vector.memset, dict(ap=self.ones_bf.ap, constant=1.0), wr=[self.ones_bf])
        self.op("dve", nc.vector.memset, dict(ap=self.ones_f.ap, constant=1.0), wr=[self.ones_f])
        self.op("dve", nc.vector.memset, dict(ap=self.epsb.ap, constant=EPS), wr=[self.epsb])
        self.ident = self.cst_t[:, 0:128]

        st = self.stages
        if st["l1"]:
            self.l1_setup()
        if st["l0"]:
            self.l0_setup()
        ntile = 2048 // TP
        if st["l0"]:
            for t in range(ntile):
                self.load_x(self.xp, t)
                self.l0_mixer(prefix=True, first_own=False)
        for t in range(ntile):
            self.load_x(self.xo, t)
            if st["l0"]:
                self.l0_mixer(prefix=False, first_own=(t == 0))
            if st["ffn0"]:
                self.ffn(0, 1)
            if st["l1"]:
                self.l1_mixer()
            if st["ffn1"]:
                self.ffn(1, 3)
            self.final(t, st["fnorm"])
        for i, sem in enumerate(self.dq["sp"]):
            n = self.dqn["sp"]
            cnt = (n // self.NS) + (1 if i < n % self.NS else 0)
            if cnt > 0:
                self._wait("sp", (sem, 16 * cnt, None))
        self.es.close()
        return nc

    def load_x(self, xd, t):
        nc = self.nc
        for blk in range(TP // 128):
            stg = self.xstg[0]
            r0 = t * TP + blk * 128
            self.dma("sp", stg.ap, xd[r0:r0 + 128, :], wr=[stg])
            for g4 in range(4):
                bk = self.bank()
                for j in range(4):
                    dt_ = g4 * 4 + j
                    self.op("pe", nc.tensor.transpose, dict(out=bk.ap[:, j * 128:(j + 1) * 128], in_=stg.ap[:, dt_ * 128:(dt_ + 1) * 128], identity=self.ident),
                            rd=[stg, self.cst], wr=[bk], mark=(j == 3))
                dst = self.hT_t[:, g4 * 4:(g4 + 1) * 4, blk * 128:(blk + 1) * 128]
                src = bk.ap.rearrange("p (a b) -> p a b", a=4)
                self.cp("act" if g4 % 2 == 0 else "dve", dst, src, rd=[bk], wr=self.hT[g4 * 4:(g4 + 1) * 4])

    def rstd_from_bank(self, bk, n, scale, P=128):
        self.act(self.rs.ap[0:P, 0:n], bk.ap[0:P, 0:n], AF.Sqrt, rd=[bk, self.epsb], wr=[self.rs], scale=scale, bias=self.eps_t[0:P, :])
        self.op("dve", self.nc.vector.reciprocal, dict(out=self.rs.ap[0:P, 0:n], in_=self.rs.ap[0:P, 0:n]), rd=[self.rs], wr=[self.rs])

    def rmsnorm(self, gi, outs):
        for hf in range(NH):
            sl = slice(hf * 512, (hf + 1) * 512)
            bk = self.bank()
            for dt_ in range(16):
                sq = self.sq[dt_ % 2]
                self.act(sq.ap, self.hT[dt_].ap[:, sl], AF.Square, rd=[self.hT[dt_]], wr=[sq])
                self.mm(bk.ap, self.ones_bf.ap, sq.ap, dt_ == 0, dt_ == 15, rd=[sq, self.ones_bf], wr=[bk], mark=True)
            self.rstd_from_bank(bk, 512, 1.0 / D)
            for dt_ in range(16):
                self.stt(outs[dt_].ap[:, sl], self.hT[dt_].ap[:, sl], self.norms_t[:, gi, dt_:dt_ + 1], self.rs.ap, ALU.mult, ALU.mult,
                         rd=[self.hT[dt_], self.rs, self.norms], wr=[outs[dt_]])

    def proj_fm(self, w, KT, rhs_tiles, hf, M=128, c0=0):
        bk = self.bank()
        w3 = w.ap.rearrange("p (k c) -> p k c", k=KT)
        for kt in range(KT):
            self.mm(bk.ap[0:M, :], w3[:, kt, c0:c0 + M], rhs_tiles[kt].ap[:, hf * 512:(hf + 1) * 512], kt == 0, kt == KT - 1,
                    rd=[w, rhs_tiles[kt]], wr=[bk], mark=(kt == KT - 1))
        return bk

    def outproj(self, wd, src):
        for db in range(16):
            w = self.wload(wd[db], 2048)
            for hf in range(NH):
                sl = slice(hf * 512, (hf + 1) * 512)
                bk = self.proj_fm(w, 16, src, hf)
                self.tt(self.hT[db].ap[:, sl], self.hT[db].ap[:, sl], bk.ap, ALU.add, rd=[bk, self.hT[db]], wr=[self.hT[db]])

    def ffn(self, layer, gi):
        self.rmsnorm(gi, self.hnT)
        for r in range(4):
            for f11 in range(11):
                fb = r * 11 + f11
                wg = self.wload(self.wg_d[layer, fb], 2048)
                wu = self.wload(self.wu_d[layer, fb], 2048)
                for hf in range(NH):
                    sl = slice(hf * 512, (hf + 1) * 512)
                    bg = self.proj_fm(wg, 16, self.hnT, hf)
                    bu = self.proj_fm(wu, 16, self.hnT, hf)
                    sg = self.sg[self.nsg % 2]
                    self.nsg += 1
                    self.act(sg.ap, bg.ap, AF.Silu, rd=[bg], wr=[sg])
                    self.tt(self.actT[f11].ap[:, sl], sg.ap, bu.ap, ALU.mult, rd=[sg, bu], wr=[self.actT[f11]])
            for db in range(16):
                w = self.wload(self.wd_d[layer, db][:, r * 1408:(r + 1) * 1408], 1408)
                for hf in range(NH):
                    sl = slice(hf * 512, (hf + 1) * 512)
                    bk = self.proj_fm(w, 11, self.actT, hf)
                    self.tt(self.hT[db].ap[:, sl], self.hT[db].ap[:, sl], bk.ap, ALU.add, rd=[bk, self.hT[db]], wr=[self.hT[db]])

    def final(self, t, do_norm):
        nc = self.nc
        if do_norm:
            self.rmsnorm(4, self.hT)
        for blk in range(TP // 128):
            stg = self.xstg[0]
            for g4 in range(4):
                bk = self.bank()
                for j in range(4):
                    dt_ = g4 * 4 + j
                    self.op("pe", nc.tensor.transpose, dict(out=bk.ap[:, j * 128:(j + 1) * 128], in_=self.hT[dt_].ap[:, blk * 128:(blk + 1) * 128], identity=self.ident),
                            rd=[self.hT[dt_], self.cst], wr=[bk], mark=(j == 3))
                self.cp("act" if g4 % 2 == 0 else "dve", stg.ap[:, g4 * 512:(g4 + 1) * 512], bk.ap, rd=[bk], wr=[stg])
            r0 = t * TP + blk * 128
            self.dma("sp", self.out_d[r0:r0 + 128, :], stg.ap, rd=[stg])

    def l1_setup(self):
        nc = self.nc
        self.lngb_t = self.sb("lngb", [128, 2, 16], F32)
        self.lngb = Buf(self.lngb_t[:])
        self.dma("sp", self.lngb.ap, self.lngb_d, wr=[self.lngb])
        self.wsf_t = self.hT_t[:, 0:2, :].rearrange("p a (b c) -> p (a b) c", c=128)
        self.wsf = self.hT[0]
        self.dma("sp", self.wsf_t, self.wsT_d, wr=[self.hT[0], self.hT[1]])
        self.bsb_t = self.hT_t[:, 2:4, :].rearrange("p a (b c) -> p (a b) c", c=128)
        self.bsb = self.hT[2]
        self.dma("sp", self.bsb_t, self.bs_d, wr=[self.hT[2], self.hT[3]])
        self.wsb_t = self.sb("wsb", [128, 8, 128], BF16)
        self.wsb = Buf(self.wsb_t[:])
        self.C2_t = self.sb("C2", [128, 16, 128], F32)
        self.C2 = Buf(self.C2_t[:])
        self.junk = self.xstg[0]
        self.st_t = self.sb("lnst", [128, 8], F32)
        self.st = Buf(self.st_t[:])
        for g in range(8):
            self.tt(self.wsf_t[:, g, :], self.wsf_t[:, g, :], self.cst_t[:, 704:832], ALU.mult, rd=[self.cst, self.hT[0], self.hT[1]], wr=[self.hT[0], self.hT[1]])
        self.cp("dve", self.wsb.ap, self.wsf_t, rd=[self.hT[0], self.hT[1]], wr=[self.wsb])
        for half in range(2):
            bk = self.bank()
            self.mm(bk.ap, self.ones_f.ap, self.wsf_t[:, half * 4:(half + 1) * 4, :], True, True, rd=[self.ones_f, self.hT[0], self.hT[1]], wr=[bk], mark=True)
            for gg in range(4):
                g = half * 4 + gg
                for cc in range(2):
                    ct = g * 2 + cc
                    self.stt(self.C2_t[:, ct, :], bk.ap[:, gg * 128:(gg + 1) * 128], self.lngb_t[:, 1, ct:ct + 1], self.bsb_t[:, g, :], ALU.mult, ALU.add,
                             rd=[bk, self.lngb, self.hT[2], self.hT[3]], wr=[self.C2])

    def l1_mixer(self):
        nc = self.nc
        self.rmsnorm(2, self.hnT)
        for ct in range(16):
            w = self.wload(self.oddu_d[ct], 2048)
            for hf in range(NH):
                bk = self.proj_fm(w, 16, self.hnT, hf)
                self.act(self.mixT[ct].ap[:, hf * 512:(hf + 1) * 512], bk.ap, AF.Gelu, rd=[bk], wr=[self.mixT[ct]])
        st = self.st_t
        for q2 in range(TP // 256):
            for cb in range(8):
                w = self.wload(self.oddv_d[cb], 4096)
                w3 = w.ap.rearrange("p (k c) -> p k c", k=16)
                for c2 in range(2):
                    ck = q2 * 2 + c2
                    bk = self.bank()
                    for kt in range(16):
                        self.mm(bk.ap[:, 0:256], self.hnT[kt].ap[:, ck * 128:(ck + 1) * 128], w3[:, kt, :], kt == 0, kt == 15, rd=[w, self.hnT[kt]], wr=[bk], mark=(kt == 15))
                    self.act(self.vg[c2].ap[:, cb * 256:(cb + 1) * 256], bk.ap[:, 0:256], AF.Gelu, rd=[bk], wr=[self.vg[c2]])
            for c2 in range(2):
                v = self.vg[c2]
                self.op("dve", nc.vector.tensor_reduce, dict(out=st[:, 0:1], in_=v.ap, axis=AX.X, op=ALU.add), rd=[v], wr=[self.st])
                self.op("dve", nc.vector.scalar_tensor_tensor, dict(out=self.junk.ap, in0=v.ap, scalar=1.0, in1=v.ap, op0=ALU.mult, op1=ALU.mult, accum_out=st[:, 1:2]),
                        rd=[v], wr=[self.junk, self.st])
                self.op("dve", nc.vector.tensor_scalar, dict(out=st[:, 2:3], in0=st[:, 0:1], scalar1=1.0 / 2048, scalar2=None, op0=ALU.mult), rd=[self.st], wr=[self.st], selfsync=True)
                self.tt(st[:, 3:4], st[:, 2:3], st[:, 2:3], ALU.mult, rd=[self.st], wr=[self.st])
                self.stt(st[:, 4:5], st[:, 1:2], 1.0 / 2048, st[:, 3:4], ALU.mult, ALU.subtract, rd=[self.st], wr=[self.st])
                self.act(st[:, 5:6], st[:, 4:5], AF.Sqrt, rd=[self.st, self.epsb], wr=[self.st], bias=self.eps_t[:, :])
                self.op("dve", nc.vector.reciprocal, dict(out=st[:, 6:7], in_=st[:, 5:6]), rd=[self.st], wr=[self.st])
                self.ts(v.ap, v.ap, st[:, 2:3], st[:, 6:7], ALU.subtract, ALU.mult, rd=[v, self.st], wr=[v])
            sl = slice(q2 * 256, (q2 + 1) * 256)
            for ct in range(16):
                bk = self.bank()
                for c2 in range(2):
                    self.mm(bk.ap[:, c2 * 128:(c2 + 1) * 128], self.vg[c2].ap[:, ct * 128:(ct + 1) * 128], self.wsb_t[:, ct // 2, :], True, True,
                            rd=[self.vg[c2], self.wsb], wr=[bk], mark=(c2 == 1))
                sg = self.sg[self.nsg % 2]
                self.nsg += 1
                for c2 in range(2):
                    cs = slice(c2 * 128, (c2 + 1) * 128)
                    self.stt(sg.ap[:, cs], bk.ap[:, cs], self.lngb_t[:, 0, ct:ct + 1], self.C2_t[:, ct, :], ALU.mult, ALU.add,
                             rd=[bk, self.lngb, self.C2], wr=[sg])
                self.tt(self.mixT[ct].ap[:, sl], self.mixT[ct].ap[:, sl], sg.ap[:, 0:256], ALU.mult, rd=[sg, self.mixT[ct]], wr=[self.mixT[ct]])
        self.outproj(self.wout1_d, self.mixT)

    def l0_setup(self):
        nc = self.nc
        f = F32
        self.hp_t = self.sb("hp", [128, 33], f)
        self.hp = Buf(self.hp_t[:])
        self.dma("sp", self.hp.ap, self.hp_d, wr=[self.hp])
        self.am0_t = self.sb("am0", [128, 128], f)
        self.am0 = Buf(self.am0_t[:])
        self.dma("sp", self.am0.ap, self.am0_d, wr=[self.am0])
        self.conv_t = self.sb("convw", [128, 24, 4], f)
        self.convw = Buf(self.conv_t[:])
        self.dma("sp", self.conv_t[:], self.conv_d, wr=[self.convw])
        self.am4_t = self.sb("am4", [128, 4, 2, 128], BF16)
        self.am4 = Buf(self.am4_t[:])
        self.am4f_t = self.sb("am4f", [128, 4, 2, 128], BF16)
        self.am4f = Buf(self.am4f_t[:])
        for i in range(4):
            self.cp("dve", self.am4_t[:, i, 0, :], self.cst_t[:, 576:704], rd=[self.cst], wr=[self.am4])
            self.cp("dve", self.am4_t[:, i, 1, :], self.cst_t[:, 704:832], rd=[self.cst], wr=[self.am4])
            self.cp("dve", self.am4f_t[:, i, 0, :], self.am0_t[:, :], rd=[self.am0], wr=[self.am4f])
            self.cp("dve", self.am4f_t[:, i, 1, :], self.cst_t[:, 704:832], rd=[self.cst], wr=[self.am4f])
        self.es_t = self.sb("es", [128, 16], f)
        self.esx = Buf(self.es_t[:])
        self.act(self.es_t[:], self.hp_t[:, 16:32], AF.Exp, rd=[self.hp], wr=[self.esx])
        nblk = TP // 128
        self.vtok_t = self.sb("vtok", [128, nblk + 1, 2, 128], BF16)
        self.vtok = Buf(self.vtok_t[:])
        self.op("dve", nc.vector.memset, dict(ap=self.vtok_t[:], constant=0.0), wr=[self.vtok])
        self.KT_t = self.sb("KT", [128, 2, 128 + TP], BF16)
        self.KT = [Buf(self.KT_t[:, i, :]) for i in range(2)]
        self.op("dve", nc.vector.memset, dict(ap=self.KT_t[:], constant=0.0), wr=self.KT)
        self.qTa = self.actT[0:8]
        self.P_t = self.sb("Pm", [128, 4, 2, 128], BF16)
        self.Pm = Buf(self.P_t[:])
        pass
        self.halo_t = self.sb("halo", [128, 24, 3], f)
        self.halo = Buf(self.halo_t[:])
        self.op("dve", nc.vector.memset, dict(ap=self.halo_t[:], constant=0.0), wr=[self.halo])
        self.S_t = self.sb("S", [128, 8, 128], f)
        self.S = [Buf(self.S_t[:, i, :]) for i in range(8)]
        self.op("dve", nc.vector.memset, dict(ap=self.S_t[:], constant=0.0), wr=self.S)
        self.nea_t = self.sb("nea", [128, 8], f)
        self.nea = Buf(self.nea_t[:])
        self.act(self.nea_t[:], self.hp_t[:, 0:8], AF.Exp, rd=[self.hp], wr=[self.nea])
        self.ts(self.nea_t[:], self.nea_t[:], -1.0, None, ALU.mult, None, rd=[self.nea], wr=[self.nea])
        names = ["U3n", "I3", "Sl3", "Su3", "mlo3", "mup3"]
        cols = [384, 512, 192, 448, 256, 320]
        self.c3 = {}
        for nm, c0 in zip(names, cols):
            self.c3[nm] = (self.cst_t[0:64, c0:c0 + 64].unsqueeze(1).to_broadcast([64, 8, 64]), self.cst)

        def mk(nm, shape, dt=f):
            t = self.sb(nm, shape, dt)
            return t, Buf(t[:])
        self.g_all = mk("g_all", [64, 8, 8])
        self.b_all = mk("b_all", [64, 8, 8])
        self.t88 = mk("t88", [64, 8, 8])
        self.cin = [mk("cin", [128, 515])] * 3
        self.cacc = mk("cacc", [128, 512])
        self.gqT = mk("gqT", [128, 512])
        self.gkT = mk("gkT", [128, 512])
        self.qs = self.gqT
        self.ks = self.gkT
        self.sqf = self.cacc
        self.gvT = mk("gvT", [128, 512])
        self.zs = mk("zs", [128, 512])
        self.g3 = mk("g3", [64, 8, 64])
        self.b3 = mk("b3", [64, 8, 64])
        self.rhs2 = mk("rhs2", [64, 8, 64])
        self.rhs3 = mk("rhs3", [64, 8, 64])
        self.ghc = mk("ghc", [64, 8])
        self.tA = self.g3
        self.tB = self.rhs3
        self.Dm = mk("Dm", [64, 8, 64])
        self.DmT = mk("DmT", [64, 8, 64])
        self.eg = mk("eg", [64, 8])
        self.egl = mk("egl", [64, 8])
        self.bg = mk("bg", [64, 8])
        self.glast = mk("glast", [128, 8])
        self.Mk = [mk("Mk%d" % i, [64, 8, 64]) for i in range(2)]
        self.Nk = [mk("Nk%d" % i, [64, 8, 64]) for i in range(2)]
        self.Rk = mk("Rk", [64, 8, 64])
        self.QKm = mk("QKm", [64, 8, 64])
        self.ktok = mk("ktok", [64, 8, 128])
        self.vtk = mk("vtk", [64, 8, 128])
        self.vb = self.vtk
        self.kbg = (self.vg_t[:, 1, :].bitcast(F32)[0:64, :].rearrange("p (a b) -> p a b", a=8), self.vg[1])
        self.kdec = self.ktok
        self.u_sb = (self.vg_t[:, 0, :].bitcast(F32)[0:64, :].rearrange("p (a b) -> p a b", a=8), self.vg[0])
        self.den = self.cacc[1]
        self.wT_sb = mk("wT_sb", [128, 512])
        self.qdT = mk("qdT", [128, 512])
        self.vnew = [mk("vnew%d" % i, [64, 128]) for i in range(2)]
        self.osb = (self.cin[0][0][:, 0:512], self.cin[0][1])

    def l0_mixer(self, prefix, first_own):
        nc = self.nc
        self.rmsnorm(0, self.hnT)
        nblk = TP // 128
        for kvh in range(2):
            w = self.wload(self.win_d[8 + kvh], 2048)
            for hf in range(NH):
                bk = self.proj_fm(w, 16, self.hnT, hf)
                self.cp("act", self.KT[kvh].ap[:, 128 + hf * 512:128 + (hf + 1) * 512], bk.ap, rd=[bk], wr=[self.KT[kvh]])
        wv = self.wload(self.win_d[10], 2048)
        wv3 = wv.ap.rearrange("p (k c) -> p k c", k=16)
        for blk in range(nblk):
            bk = self.bank()
            for kt in range(16):
                self.mm(bk.ap[:, 0:128], self.hnT[kt].ap[:, blk * 128:(blk + 1) * 128], wv3[:, kt, :], kt == 0, kt == 15, rd=[wv, self.hnT[kt]], wr=[bk], mark=(kt == 15))
            self.cp("dve", self.vtok_t[:, 1 + blk, :, 64:128], bk.ap[:, 0:128].rearrange("p (a b) -> p a b", a=2), rd=[bk], wr=[self.vtok])
        if not prefix:
            for ct in range(8):
                w = self.wload(self.win_d[ct], 2048)
                for hf in range(NH):
                    bk = self.proj_fm(w, 16, self.hnT, hf)
                    self.cp("act", self.qTa[ct].ap[:, hf * 512:(hf + 1) * 512], bk.ap, rd=[bk], wr=[self.qTa[ct]])
            self.attention(first_own)
        for kvh in range(2):
            self.cp("dve", self.KT[kvh].ap[:, 0:128], self.KT[kvh].ap[:, TP:TP + 128], rd=[self.KT[kvh]], wr=[self.KT[kvh]])
        self.cp("dve", self.vtok_t[:, 0, :, :], self.vtok_t[:, nblk, :, :], rd=[self.vtok], wr=[self.vtok])
        wba = self.wload(self.wba_d, 256)
        for hf in range(NH):
            self.gdn_pre(wba, hf)
            for h in range(8):
                self.gdn_head(h, hf, prefix)
        if not prefix:
            self.outproj(self.wout0_d, self.mixT)

    def attention(self, first_own):
        nc = self.nc
        Pt = self.P_t
        for blk in range(TP // 128):
            am = self.am4f if (first_own and blk == 0) else self.am4
            ts_ = slice(blk * 128, (blk + 1) * 128)
            for kvh in range(2):
                for par in range(2):
                    pr = slice(par * 64, par * 64 + 64)
                    b0 = self.bank()
                    b1 = self.bank()
                    for i in range(4):
                        ct = kvh * 4 + i
                        bk = b0 if i < 2 else b1
                        for j in range(2):
                            kcols = slice(blk * 128 + j * 128, blk * 128 + (j + 1) * 128)
                            oc = ((i % 2) * 2 + j) * 128
                            self.mm(bk.ap[:, oc:oc + 128], self.KT[kvh].ap[pr, kcols], self.qTa[ct].ap[pr, ts_], True, True,
                                    rd=[self.KT[kvh], self.qTa[ct]], wr=[bk], mark=(i % 2 == 1 and j == 1))
                    self.act(Pt[:, 0:2, :, :], b0.ap.rearrange("p (a b c) -> p a b c", a=2, b=2), AF.Exp, rd=[b0], wr=[self.Pm], scale=0.125)
                    self.act(Pt[:, 2:4, :, :], b1.ap.rearrange("p (a b c) -> p a b c", a=2, b=2), AF.Exp, rd=[b1], wr=[self.Pm], scale=0.125)
                    self.tt(Pt[:], Pt[:], am.ap, ALU.mult, rd=[self.Pm, am], wr=[self.Pm])
                    bo = self.bank()
                    bs = self.bank()
                    M = 64 if par == 0 else 128
                    for j in range(2):
                        vl = self.vtok_t[:, blk + j, kvh, 64:128] if par == 0 else self.vtok_t[:, blk + j, kvh, :]
                        self.mm(bo.ap[0:M, :], vl, Pt[:, :, j, :], j == 0, j == 1, rd=[self.vtok, self.Pm], wr=[bo], mark=(j == 1))
                    for j in range(2):
                        self.mm(bs.ap[0:M, :], self.ones_t[:, 0:M], Pt[:, :, j, :], j == 0, j == 1, rd=[self.ones_bf, self.Pm], wr=[bs], mark=(j == 1))
                    h0 = kvh * 8 + par
                    dn = self.den.ap[pr, :]
                    for i in range(4):
                        hh = h0 + 2 * i
                        self.ts(dn[:, i * 128:(i + 1) * 128], bs.ap[pr, i * 128:(i + 1) * 128], self.es_t[pr, hh:hh + 1], None, ALU.add, None, rd=[bs, self.esx], wr=[self.den])
                    self.op("dve", nc.vector.reciprocal, dict(out=dn, in_=dn), rd=[self.den], wr=[self.den])
                    dst = self.mix_t[pr, kvh * 4:(kvh + 1) * 4, ts_]
                    self.tt(dst, bo.ap[pr, :].rearrange("p (a b) -> p a b", a=4), dn.rearrange("p (a b) -> p a b", a=4), ALU.mult,
                            rd=[bo, self.den], wr=self.mixT[kvh * 4:(kvh + 1) * 4])

    def gdn_pre(self, wba, hf):
        nc = self.nc
        bk = self.banks[7]
        w3 = wba.ap.rearrange("p (k c) -> p k c", k=16)
        for c in range(8):
            t0 = hf * 512 + c * 64
            for kt in range(16):
                self.mm(bk.ap[0:64, c * 16:(c + 1) * 16], self.hnT[kt].ap[:, t0:t0 + 64], w3[:, kt, :], kt == 0, kt == 15,
                        rd=[wba, self.hnT[kt]], wr=[bk], mark=(kt == 15 and c == 7))
        raw = bk.ap[0:64, 0:128].rearrange("p (c k) -> p c k", c=8)
        self.act(self.b_all[0][:], raw[:, :, 0:8], AF.Sigmoid, rd=[bk], wr=[self.b_all[1]])
        dtb = self.hp_t[0:64, 8:16].unsqueeze(1).to_broadcast([64, 8, 8])
        self.tt(self.t88[0][:], raw[:, :, 8:16], dtb, ALU.add, rd=[bk, self.hp], wr=[self.t88[1]])
        self.act(self.t88[0][:], self.t88[0][:], AF.Exp, rd=[self.t88[1]], wr=[self.t88[1]])
        self.act(self.t88[0][:], self.t88[0][:], AF.Ln, rd=[self.t88[1], self.ones_f], wr=[self.t88[1]], bias=self.onesf_t[0:64, 0:1])
        nea = self.nea_t[0:64, :].unsqueeze(1).to_broadcast([64, 8, 8])
        self.tt(self.g_all[0][:], self.t88[0][:], nea, ALU.mult, rd=[self.t88[1], self.nea], wr=[self.g_all[1]])

    def gdn_head(self, h, hf, prefix):
        nc = self.nc
        B = self.banks
        sl = slice(hf * 512, (hf + 1) * 512)
        f64 = self.onesf_t[0:64, 0:64]
        f128 = self.onesf_t[0:64, 0:128]
        U64 = self.cst_t[0:64, 128:192]
        L64 = self.cst_t[0:64, 192:256]
        C = lambda t: t[0]
        outs = [self.qs, self.ks, self.gvT]
        for j in range(3 if True else 0):
            w = self.wload(self.win_d[11 + h * 4 + j], 2048)
            bk = self.proj_fm(w, 16, self.hnT, hf)
            ct = j * 8 + h
            cin_t, cin_b = self.cin[j]
            self.cp("act", cin_t[:, 3:515], bk.ap, rd=[bk], wr=[cin_b])
            self.cp("dve", cin_t[:, 0:3], self.halo_t[:, ct, :], rd=[self.halo], wr=[cin_b])
            acc_t, acc_b = self.cacc
            cw = self.conv_t
            self.ts(acc_t[:], cin_t[:, 3:515], cw[:, ct, 3:4], None, ALU.mult, None, rd=[cin_b, self.convw], wr=[acc_b])
            for kk in (2, 1, 0):
                self.stt(acc_t[:], cin_t[:, kk:kk + 512], cw[:, ct, kk:kk + 1], acc_t[:], ALU.mult, ALU.add, rd=[cin_b, self.convw, acc_b], wr=[acc_b])
            self.cp("dve", self.halo_t[:, ct, :], cin_t[:, 512:515], rd=[cin_b], wr=[self.halo])
            self.act(outs[j][0][:], acc_t[:], AF.Silu, rd=[acc_b], wr=[outs[j][1]])
        if not prefix:
            w = self.wload(self.win_d[11 + h * 4 + 3], 2048)
            bk = self.proj_fm(w, 16, self.hnT, hf)
            self.act(self.zs[0][:], bk.ap, AF.Silu, rd=[bk], wr=[self.zs[1]])
        for (src, dst, scl) in ((self.qs, self.gqT, 128 ** -0.5), (self.ks, self.gkT, 1.0)):
            self.act(self.sqf[0][:], src[0][:], AF.Square, rd=[src[1]], wr=[self.sqf[1]])
            bk = self.bank()
            self.mm(bk.ap, self.onesf_t[:, :], self.sqf[0][:], True, True, rd=[self.ones_f, self.sqf[1]], wr=[bk], mark=True)
            self.rstd_from_bank(bk, 512, 1.0)
            self.stt(dst[0][:], src[0][:], scl, self.rs.ap, ALU.mult, ALU.mult, rd=[src[1], self.rs], wr=[dst[1]])
        qT, kT, vT = self.gqT, self.gkT, self.gvT
        g_h = self.g_all[0][:, :, h]
        b_h = self.b_all[0][:, :, h]
        self.cp("dve", self.ghc[0][:], g_h, rd=[self.g_all[1]], wr=[self.ghc[1]])
        self.cp("dve", self.g3[0][:], g_h.unsqueeze(2).to_broadcast([64, 8, 64]), rd=[self.g_all[1]], wr=[self.g3[1]])
        self.cp("dve", self.b3[0][:], b_h.unsqueeze(2).to_broadcast([64, 8, 64]), rd=[self.b_all[1]], wr=[self.b3[1]])
        self.tt(self.rhs2[0][:], self.g3[0][:], self.c3["U3n"][0], ALU.mult, rd=[self.g3[1], self.c3["U3n"][1]], wr=[self.rhs2[1]])
        self.tt(self.rhs3[0][:], self.b3[0][:], self.c3["I3"][0], ALU.mult, rd=[self.b3[1], self.c3["I3"][1]], wr=[self.rhs3[1]])
        fl = lambda t: t[0][:].rearrange("p a b -> p (a b)")
        self.mm(B[0].ap[0:64, :], U64, fl(self.g3), True, False, rd=[self.cst, self.g3[1]], wr=[B[0]], mark=False)
        self.mm(B[0].ap[0:64, :], f64, fl(self.rhs2), False, True, rd=[self.ones_f, self.rhs2[1]], wr=[B[0]], mark=True)
        self.mm(B[1].ap[0:64, :], f64, fl(self.rhs3), True, True, rd=[self.ones_f, self.rhs3[1]], wr=[B[1]], mark=True)
        self.mm(B[2].ap[0:64, 0:8], U64, self.ghc[0][:], True, True, rd=[self.cst, self.ghc[1]], wr=[B[2]], mark=False)
        self.mm(B[2].ap[0:64, 8:16], L64, self.ghc[0][:], True, True, rd=[self.cst, self.ghc[1]], wr=[B[2]], mark=False)
        self.mm(B[2].ap[0:128, 16:24], f128, self.ghc[0][:], True, True, rd=[self.ones_f, self.ghc[1]], wr=[B[2]], mark=True)
        if not prefix:
            self.mm(B[3].ap[:, :], f128, fl(self.rhs2), True, True, rd=[self.ones_f, self.rhs2[1]], wr=[B[3]], mark=True)
        for c in range(8):
            cs = slice(c * 64, (c + 1) * 64)
            self.mm(B[4].ap[0:64, cs], kT[0][:, cs], kT[0][:, cs], True, True, rd=[kT[1]], wr=[B[4]], mark=(c == 7))
        if not prefix:
            for c in range(8):
                cs = slice(c * 64, (c + 1) * 64)
                self.mm(B[5].ap[0:64, cs], kT[0][:, cs], qT[0][:, cs], True, True, rd=[kT[1], qT[1]], wr=[B[5]], mark=(c == 7))
        v3 = lambda ap: ap.rearrange("p (a b) -> p a b", a=8)
        D3 = v3(B[0].ap[0:64, :])
        self.tt(self.tA[0][:], D3, self.c3["mlo3"][0], ALU.add, rd=[B[0], self.c3["mlo3"][1]], wr=[self.tA[1]])
        self.act(self.Dm[0][:], self.tA[0][:], AF.Exp, rd=[self.tA[1]], wr=[self.Dm[1]])
        self.stt(self.tB[0][:], D3, -1.0, self.c3["mup3"][0], ALU.mult, ALU.add, rd=[B[0], self.c3["mup3"][1]], wr=[self.tB[1]])
        self.act(self.DmT[0][:], self.tB[0][:], AF.Exp, rd=[self.tB[1]], wr=[self.DmT[1]])
        self.act(self.eg[0][:], B[2].ap[0:64, 0:8], AF.Exp, rd=[B[2]], wr=[self.eg[1]])
        self.act(self.egl[0][:], B[2].ap[0:64, 8:16], AF.Exp, rd=[B[2]], wr=[self.egl[1]])
        self.act(self.glast[0][:], B[2].ap[0:128, 16:24], AF.Exp, rd=[B[2]], wr=[self.glast[1]])
        if not prefix:
            self.act(self.qdT[0][:], B[3].ap[:, :], AF.Exp, rd=[B[3]], wr=[self.qdT[1]], scale=-1.0)
        KK3 = v3(B[4].ap[0:64, :])
        self.stt(self.tA[0][:], self.Dm[0][:], -1.0, self.c3["Sl3"][0], ALU.mult, ALU.mult, rd=[self.Dm[1], self.c3["Sl3"][1]], wr=[self.tA[1]])
        self.tt(self.tA[0][:], self.tA[0][:], self.b3[0][:], ALU.mult, rd=[self.tA[1], self.b3[1]], wr=[self.tA[1]])
        self.tt(self.Nk[0][0][:], KK3, self.tA[0][:], ALU.mult, rd=[B[4], self.tA[1]], wr=[self.Nk[0][1]])
        self.stt(self.tB[0][:], self.DmT[0][:], -1.0, self.c3["Su3"][0], ALU.mult, ALU.mult, rd=[self.DmT[1], self.c3["Su3"][1]], wr=[self.tB[1]])
        self.tt(self.tB[0][:], self.tB[0][:], v3(B[1].ap[0:64, :]), ALU.mult, rd=[self.tB[1], B[1]], wr=[self.tB[1]])
        self.tt(self.Mk[0][0][:], KK3, self.tB[0][:], ALU.mult, rd=[B[4], self.tB[1]], wr=[self.Mk[0][1]])
        if not prefix:
            self.tt(self.QKm[0][:], v3(B[5].ap[0:64, :]), self.DmT[0][:], ALU.mult, rd=[B[5], self.DmT[1]], wr=[self.QKm[1]])
        self.tt(self.Rk[0][:], self.Mk[0][0][:], self.c3["I3"][0], ALU.add, rd=[self.Mk[0][1], self.c3["I3"][1]], wr=[self.Rk[1]])
        for (src, dst, b0) in ((kT, self.ktok, 6), (vT, self.vtk, 0)):
            for c in range(8):
                bk = B[b0 + c // 4]
                self.op("pe", nc.tensor.transpose, dict(out=bk.ap[0:64, (c % 4) * 128:(c % 4 + 1) * 128], in_=src[0][:, c * 64:(c + 1) * 64], identity=self.ident),
                        rd=[src[1], self.cst], wr=[bk], mark=(c % 4 == 3))
            for hh in range(2):
                self.cp("act", dst[0][:, hh * 4:(hh + 1) * 4, :], B[b0 + hh].ap[0:64, :].rearrange("p (a b) -> p a b", a=4), rd=[B[b0 + hh]], wr=[dst[1]])
        cur = 0
        for lvl in range(1, 6):
            nxt = 1 - cur
            Mc, Nc = self.Mk[cur], self.Nk[cur]
            Mn, Nn = self.Mk[nxt], self.Nk[nxt]
            if lvl < 5:
                for c in range(8):
                    cs = slice(c * 64, (c + 1) * 64)
                    self.mm(B[2].ap[0:64, cs], Nc[0][:, c, :], Mc[0][:, c, :], True, True, rd=[Nc[1], Mc[1]], wr=[B[2]], mark=(c == 7))
            for c in range(8):
                cs = slice(c * 64, (c + 1) * 64)
                self.mm(B[3].ap[0:64, cs], Mc[0][:, c, :], Nc[0][:, c, :], True, True, rd=[Nc[1], Mc[1]], wr=[B[3]], mark=(c == 7))
            if lvl < 5:
                self.cp("act", Mn[0][:], v3(B[2].ap[0:64, :]), rd=[B[2]], wr=[Mn[1]])
            self.cp("dve", Nn[0][:], v3(B[3].ap[0:64, :]), rd=[B[3]], wr=[Nn[1]])
            for c in range(8):
                cs = slice(c * 64, (c + 1) * 64)
                self.mm(B[4].ap[0:64, cs], Nn[0][:, c, :], self.Rk[0][:, c, :], True, True, rd=[Nn[1], self.Rk[1]], wr=[B[4]], mark=(c == 7))
            self.tt(self.Rk[0][:], self.Rk[0][:], v3(B[4].ap[0:64, :]), ALU.add, rd=[B[4], self.Rk[1]], wr=[self.Rk[1]])
            cur = nxt
        self.tt(self.bg[0][:], b_h, self.eg[0][:], ALU.mult, rd=[self.b_all[1], self.eg[1]], wr=[self.bg[1]])
        self.tt(self.vb[0][:], self.vtk[0][:], b_h.unsqueeze(2).to_broadcast([64, 8, 128]), ALU.mult, rd=[self.vtk[1], self.b_all[1]], wr=[self.vb[1]])
        self.tt(self.kbg[0][:], self.ktok[0][:], self.bg[0][:].unsqueeze(2).to_broadcast([64, 8, 128]), ALU.mult, rd=[self.ktok[1], self.bg[1]], wr=[self.kbg[1]])
        self.tt(self.kdec[0][:], self.ktok[0][:], self.egl[0][:].unsqueeze(2).to_broadcast([64, 8, 128]), ALU.mult, rd=[self.ktok[1], self.egl[1]], wr=[self.kdec[1]])
        for c in range(8):
            bk = B[5 + c // 4]
            self.mm(bk.ap[0:64, (c % 4) * 128:(c % 4 + 1) * 128], self.Rk[0][:, c, :], self.vb[0][:, c, :], True, True, rd=[self.Rk[1], self.vb[1]], wr=[bk], mark=(c % 4 == 3))
        for hh in range(2):
            self.cp("act", self.u_sb[0][:, hh * 4:(hh + 1) * 4, :], B[5 + hh].ap[0:64, :].rearrange("p (a b) -> p a b", a=4), rd=[B[5 + hh]], wr=[self.u_sb[1]])
        for c in range(8):
            cs = slice(c * 64, (c + 1) * 64)
            self.mm(B[7].ap[:, cs], self.kbg[0][:, c, :], self.Rk[0][:, c, :], True, True, rd=[self.kbg[1], self.Rk[1]], wr=[B[7]], mark=(c == 7))
        self.cp("dve", self.wT_sb[0][:], B[7].ap[:, :], rd=[B[7]], wr=[self.wT_sb[1]])
        if not prefix:
            self.tt(self.qdT[0][:], qT[0][:], self.qdT[0][:], ALU.mult, rd=[qT[1], self.qdT[1]], wr=[self.qdT[1]])
        S = self.S[h]
        for c in range(8):
            cs = slice(c * 64, (c + 1) * 64)
            vb_ = B[1 + c % 2]
            self.mm(vb_.ap[0:64, 0:128], self.wT_sb[0][:, cs], S.ap, True, True, rd=[self.wT_sb[1], S], wr=[vb_], mark=True)
            vn = self.vnew[c % 2]
            self.tt(vn[0][:], self.u_sb[0][:, c, :], vb_.ap[0:64, 0:128], ALU.subtract, rd=[self.u_sb[1], vb_], wr=[vn[1]])
            if not prefix:
                self.mm(B[0].ap[:, cs], S.ap, self.qdT[0][:, cs], True, False, rd=[S, self.qdT[1]], wr=[B[0]], mark=False)
                self.mm(B[0].ap[:, cs], vn[0][:], self.QKm[0][:, c, :], False, True, rd=[vn[1], self.QKm[1]], wr=[B[0]], mark=False)
            sb_ = B[3 + c % 2]
            self.mm(sb_.ap[:, 0:128], self.kdec[0][:, c, :], vn[0][:], True, True, rd=[self.kdec[1], vn[1]], wr=[sb_], mark=True)
            self.stt(S.ap, S.ap, self.glast[0][:, c:c + 1], sb_.ap[:, 0:128], ALU.mult, ALU.add, rd=[S, self.glast[1], sb_], wr=[S])
        if not prefix:
            self.cp("act", self.osb[0][:], B[0].ap[:, :], rd=[B[0]], wr=[self.osb[1]])
            self.act(self.sqf[0][:], self.osb[0][:], AF.Square, rd=[self.osb[1]], wr=[self.sqf[1]])
            bk = B[5]
            self.mm(bk.ap, self.onesf_t[:, :], self.sqf[0][:], True, True, rd=[self.ones_f, self.sqf[1]], wr=[bk], mark=True)
            self.rstd_from_bank(bk, 512, 1.0 / 128)
            self.stt(self.osb[0][:], self.osb[0][:], self.hp_t[:, 32:33], self.rs.ap, ALU.mult, ALU.mult, rd=[self.osb[1], self.hp, self.rs], wr=[self.osb[1]])
            self.tt(self.mixT[8 + h].ap[:, sl], self.osb[0][:], self.zs[0][:], ALU.mult, rd=[self.osb[1], self.zs[1]], wr=[self.mixT[8 + h]])


def blockify(Wc):
    K, C = Wc.shape
    return np.ascontiguousarray(Wc.reshape(K // 128, 128, C).transpose(1, 0, 2)).reshape(128, (K // 128) * C)


def col16(v):
    return np.ascontiguousarray(v.reshape(16, 128).T)


def make_consts():
    c = np.zeros((128, NCST), np.float32)
    c[:, 0:128] = np.eye(128, dtype=np.float32)
    i = np.arange(64)
    c[0:64, 128:192] = (i[:, None] <= i[None, :])
    c[0:64, 192:256] = (i[:, None] > i[None, :])
    c[0:64, 256:320] = np.where(i[:, None] >= i[None, :], 0.0, -1e4)
    c[0:64, 320:384] = np.where(i[None, :] >= i[:, None], 0.0, -1e4)
    c[0:64, 384:448] = -(i[:, None] <= i[None, :]).astype(np.float32)
    c[0:64, 448:512] = (i[None, :] > i[:, None])
    c[0:64, 512:576] = np.eye(64)
    k = np.arange(128)
    c[:, 576:704] = (k[:, None] > k[None, :])
    c[:, 704:832] = (k[:, None] <= k[None, :])
    return c


STAGES = dict(l0=True, ffn0=True, l1=True, ffn1=True, fnorm=True)
_NC_CACHE = {}


def prep_shared(inp):
    f = np.float32
    sh = {}
    sh["cst"] = make_consts()
    norms = np.zeros((128, 6, 16), f)
    norms[:, 0] = col16(inp["even_norm"][0])
    norms[:, 1] = col16(inp["ffn_norm"][0])
    norms[:, 2] = col16(inp["odd_norm"][0])
    norms[:, 3] = col16(inp["ffn_norm"][1])
    norms[:, 4] = col16(inp["final_norm"])
    sh["norms"] = norms
    wi = inp["even_w_in"][0]
    blocks = []
    for ct in range(8):
        blocks.append(blockify(wi[:, ct * 128:(ct + 1) * 128]))
    for kvh in range(2):
        kc = wi[:, 1024 + kvh * 64:1024 + (kvh + 1) * 64]
        blocks.append(blockify(np.concatenate([kc, kc], axis=1)))
    blocks.append(blockify(wi[:, 1152:1280]))
    for h in range(8):
        for base in (1280, 2304, 3328, 4352):
            blocks.append(blockify(wi[:, base + h * 128:base + (h + 1) * 128]))
    sh["win"] = np.stack(blocks)
    sh["wba"] = blockify(wi[:, 5376:5392])
    cv = inp["even_conv"][0]
    sh["conv"] = np.ascontiguousarray(cv.T.reshape(24, 128, 4).transpose(1, 0, 2))
    hp = np.zeros((128, 33), f)
    hp[:, 0:8] = inp["even_a_log"][0][None, :]
    hp[:, 8:16] = inp["even_dt_bias"][0][None, :]
    hp[:, 16:32] = inp["even_sinks"][0][None, :]
    hp[:, 32] = inp["even_onorm"][0]
    sh["hp"] = hp
    wo = inp["even_w_out"][0]
    sh["wout0"] = np.stack([blockify(wo[:, d * 128:(d + 1) * 128]) for d in range(16)])
    ow = inp["odd_w_in"][0]
    sh["oddu"] = np.stack([blockify(ow[:, c * 128:(c + 1) * 128]) for c in range(16)])
    sh["oddv"] = np.stack([blockify(ow[:, 2048 + c * 256:2048 + (c + 1) * 256]) for c in range(8)])
    lngb = np.zeros((128, 2, 16), f)
    lngb[:, 0] = col16(inp["odd_ln_g"][0])
    lngb[:, 1] = col16(inp["odd_ln_b"][0])
    sh["lngb"] = lngb
    sh["wsT"] = np.ascontiguousarray(inp["odd_w_s"][0].transpose(2, 0, 1))
    sh["bsb"] = np.ascontiguousarray(np.broadcast_to(inp["odd_b_s"][0][None], (128, 8, 128)))
    wo1 = inp["odd_w_out"][0]
    sh["wout1"] = np.stack([blockify(wo1[:, d * 128:(d + 1) * 128]) for d in range(16)])
    sh["wg"] = np.stack([np.stack([blockify(inp["ffn_w_gate"][l][:, c * 128:(c + 1) * 128]) for c in range(44)]) for l in range(2)])
    sh["wu"] = np.stack([np.stack([blockify(inp["ffn_w_up"][l][:, c * 128:(c + 1) * 128]) for c in range(44)]) for l in range(2)])
    sh["wd"] = np.stack([np.stack([blockify(inp["ffn_w_down"][l][:, c * 128:(c + 1) * 128]) for c in range(16)]) for l in range(2)])
    return sh


def kernel(**inp):
    inp = {k: np.asarray(v) for k, v in inp.items()}
    key = tuple(sorted(STAGES.items()))
    if key not in _NC_CACHE:
        _NC_CACHE[key] = KB(dict(STAGES)).build()
    nc = _NC_CACHE[key]
    sh = prep_shared(inp)
    x = inp["x"]
    k = np.arange(128)
    su = (k[:, None] > k[None, :]).astype(np.float32)
    in_maps = []
    for c in range(8):
        b, half = c // 2, c % 2
        m = dict(sh)
        m["xo"] = np.ascontiguousarray(x[b, half * 2048:(half + 1) * 2048])
        m["xp"] = np.ascontiguousarray(x[b, 0:2048]) if half == 1 else np.zeros((2048, D), np.float32)
        m["am0"] = su if half == 1 else np.zeros((128, 128), np.float32)
        in_maps.append(m)
    import os
    npair = int(os.environ.get("KDEBUG_PAIRS", "4"))
    out = np.zeros((4, 4096, D), np.float32)
    for p in range(npair):
        res = run_bass_kernel_spmd(nc, in_maps[2 * p:2 * p + 2], core_ids=[0, 1])
        for j in range(2):
            c = 2 * p + j
            b, half = c // 2, c % 2
            out[b, half * 2048:(half + 1) * 2048] = res.results[j]["out"]
    return out
```

```python
import numpy as np
from contextlib import ExitStack
import concourse.bass as bass
import concourse.mybir as mybir
from concourse.bass_utils import run_bass_kernel_spmd

F32 = mybir.dt.float32
BF16 = mybir.dt.bfloat16
AF = mybir.ActivationFunctionType
ALU = mybir.AluOpType
AX = mybir.AxisListType

D = 2048
DFF = 5632
NFT = 44
EPS = 1e-6
TP = 512
NH = TP // 512
NCST = 832
import os as _os
SELFSYNC_ALL = _os.environ.get('KSELFSYNC', '0') == '1'


class Buf:
    __slots__ = ("ap", "wev", "revs")

    def __init__(self, ap):
        self.ap = ap
        self.wev = None
        self.revs = {}


class KB:
    def __init__(self, stages):
        self.stages = stages
        self.nc = bass.Bass("TRN2", target_bir_lowering=False)
        self.es = ExitStack()
        nc = self.nc
        self.engs = {"pe": nc.tensor, "act": nc.scalar, "dve": nc.vector, "pool": nc.gpsimd, "sp": nc.sync}
        self.psem = {e: self.es.enter_context(nc.semaphore("p_" + e)) for e in ("pe", "act", "dve")}
        self.pcnt = {e: 0 for e in self.psem}
        self.pending = {e: ([], []) for e in self.psem}
        self.waited = {}
        self.NS = 12
        self.dq = {q: [self.es.enter_context(nc.semaphore("d_%s_%d" % (q, i))) for i in range(self.NS)] for q in ("sp", "pool")}
        self.dqn = {"sp": 0, "pool": 0}
        self.nbank = 0

    def sb(self, name, shape, dt):
        return self.es.enter_context(self.nc.sbuf_tensor("s_" + name, list(shape), dt))

    def dram(self, name, shape, dt=F32, kind="ExternalInput"):
        return self.nc.dram_tensor(name, list(shape), dt, kind=kind).ap()

    def _wait(self, eng, ev):
        if ev is None:
            return
        sem, val, src = ev
        if src == eng and eng == "pe":
            return
        key = (eng, sem.name if hasattr(sem, "name") else id(sem))
        if self.waited.get(key, 0) >= val:
            return
        self.engs[eng].wait_ge(sem, val)
        self.waited[key] = val

    def op(self, eng, fn, kw, rd=(), wr=(), mark=True, selfsync=False):
        if (selfsync or (SELFSYNC_ALL and eng in ("act", "dve"))) and self.pcnt[eng] > 0:
            self.engs[eng].wait_ge(self.psem[eng], self.pcnt[eng])
        for b in rd:
            self._wait(eng, b.wev)
        for b in wr:
            self._wait(eng, b.wev)
            for e in list(b.revs.values()):
                self._wait(eng, e)
        inst = fn(**kw)
        pr, pw = self.pending[eng]
        pr.extend(rd)
        pw.extend(wr)
        if mark:
            self.pcnt[eng] += 1
            inst.then_inc(self.psem[eng], 1)
            ev = (self.psem[eng], self.pcnt[eng], eng)
            for b in pr:
                b.revs[eng] = ev
            for b in pw:
                b.wev = ev
                b.revs = {}
            self.pending[eng] = ([], [])
        return inst

    def dma(self, q, out_ap, in_ap, rd=(), wr=()):
        for b in rd:
            self._wait(q, b.wev)
        for b in wr:
            self._wait(q, b.wev)
            for e in list(b.revs.values()):
                self._wait(q, e)
        n = self.dqn[q]
        sem = self.dq[q][n % self.NS]
        val = 16 * (n // self.NS + 1)
        if val > 16:
            self._wait(q, (sem, val - 16, None))
        inst = self.engs[q].dma_start(out=out_ap, in_=in_ap)
        inst.then_inc(sem, 16)
        ev = (sem, val, None)
        for b in rd:
            b.revs[("dma", q, n % self.NS)] = ev
        for b in wr:
            b.wev = ev
            b.revs = {}
        self.dqn[q] = n + 1
        return ev

    def bank(self):
        b = self.banks[self.nbank % 8]
        self.nbank += 1
        return b

    def mm(self, out, lhsT, rhs, start, stop, rd, wr, mark):
        return self.op("pe", self.nc.tensor.matmul, dict(out=out, lhsT=lhsT, rhs=rhs, start=start, stop=stop), rd=rd, wr=wr, mark=mark)

    def act(self, out, in_, func, rd, wr, **kw):
        return self.op("act", self.nc.scalar.activation, dict(out=out, in_=in_, func=func, **kw), rd=rd, wr=wr)

    def tt(self, out, in0, in1, op, rd, wr):
        return self.op("dve", self.nc.vector.tensor_tensor, dict(out=out, in0=in0, in1=in1, op=op), rd=rd, wr=wr)

    def ts(self, out, in0, s1, s2, op0, op1, rd, wr):
        kw = dict(out=out, in0=in0, scalar1=s1, scalar2=s2, op0=op0)
        if op1 is not None:
            kw["op1"] = op1
        return self.op("dve", self.nc.vector.tensor_scalar, kw, rd=rd, wr=wr)

    def stt(self, out, in0, scalar, in1, op0, op1, rd, wr):
        return self.op("dve", self.nc.vector.scalar_tensor_tensor, dict(out=out, in0=in0, scalar=scalar, in1=in1, op0=op0, op1=op1), rd=rd, wr=wr)

    def cp(self, eng, out, in_, rd, wr):
        if eng == "act":
            return self.act(out, in_, AF.Copy, rd, wr)
        return self.op("dve", self.nc.vector.tensor_copy, dict(out=out, in_=in_), rd=rd, wr=wr)

    def wload(self, dram_ap, n):
        A = self.W_N
        if self.wptr + n > A:
            self.wptr = 0
        s, e = self.wptr, self.wptr + n
        self.wptr = e
        deps = [b for (a, bn, b) in self.wlive if a < e and bn > s]
        self.wlive = [(a, bn, b) for (a, bn, b) in self.wlive if not (a < e and bn > s)]
        buf = Buf(self.warena[:, s:e])
        self.dma("pool", buf.ap, dram_ap, wr=deps + [buf])
        self.wlive.append((s, e, buf))
        return buf

    def build(self):
        nc = self.nc
        kb = self
        self.xo = self.dram("xo", [2048, D])
        self.xp = self.dram("xp", [2048, D])
        self.cst_d = self.dram("cst", [128, NCST])
        self.am0_d = self.dram("am0", [128, 128])
        self.norms_d = self.dram("norms", [128, 6, 16])
        self.win_d = self.dram("win", [43, 128, 2048])
        self.wba_d = self.dram("wba", [128, 256])
        self.conv_d = self.dram("conv", [128, 24, 4])
        self.hp_d = self.dram("hp", [128, 8 + 8 + 16 + 1])
        self.wout0_d = self.dram("wout0", [16, 128, 2048])
        self.oddu_d = self.dram("oddu", [16, 128, 2048])
        self.oddv_d = self.dram("oddv", [8, 128, 4096])
        self.lngb_d = self.dram("lngb", [128, 2, 16])
        self.wsT_d = self.dram("wsT", [128, 8, 128])
        self.bs_d = self.dram("bsb", [128, 8, 128])
        self.wout1_d = self.dram("wout1", [16, 128, 2048])
        self.wg_d = self.dram("wg", [2, 44, 128, 2048])
        self.wu_d = self.dram("wu", [2, 44, 128, 2048])
        self.wd_d = self.dram("wd", [2, 16, 128, 5632])
        self.out_d = self.dram("out", [2048, D], kind="ExternalOutput")

        self.hT_t = self.sb("hT", [128, 16, TP], F32)
        self.hT = [Buf(self.hT_t[:, i, :]) for i in range(16)]
        self.hn_t = self.sb("hnT", [128, 16, TP], BF16)
        self.hnT = [Buf(self.hn_t[:, i, :]) for i in range(16)]
        self.mix_t = self.sb("mixT", [128, 16, TP], BF16)
        self.mixT = [Buf(self.mix_t[:, i, :]) for i in range(16)]
        self.act_t = self.sb("actT", [128, 11, TP], BF16)
        self.actT = [Buf(self.act_t[:, i, :]) for i in range(11)]
        self.W_N = 10240
        self.warena = self.sb("warena", [128, self.W_N], BF16)
        self.wptr = 0
        self.wlive = []
        self.cst_t = self.sb("cst", [128, NCST], F32)
        self.cst = Buf(self.cst_t[:])
        self.norms_t = self.sb("norms", [128, 6, 16], F32)
        self.norms = Buf(self.norms_t[:])
        self.ones_t = self.sb("ones_bf", [128, 128], BF16)
        self.ones_bf = Buf(self.ones_t[:])
        self.onesf_t = self.sb("ones_f", [128, 128], F32)
        self.ones_f = Buf(self.onesf_t[:])
        self.eps_t = self.sb("eps", [128, 1], F32)
        self.epsb = Buf(self.eps_t[:])
        self.sq = [Buf(self.sb("sq%d" % i, [128, 512], BF16)[:]) for i in range(2)]
        self.rs = Buf(self.sb("rs", [128, 512], F32)[:])
        self.sg = [Buf(self.sb("sg%d" % i, [128, 512], F32)[:]) for i in range(2)]
        self.xstg = [Buf(self.sb("xstg%d" % i, [128, D], F32)[:]) for i in range(1)]
        self.nsg = 0
        self.vg_t = self.sb("vg", [128, 2, 2048], BF16)
        self.vg = [Buf(self.vg_t[:, i, :]) for i in range(2)]
        ps = [self.es.enter_context(nc.psum_tensor("ps%d" % i, [128, 512], F32)) for i in range(8)]
        self.banks = [Buf(p[:]) for p in ps]

        self.dma("sp", self.cst.ap, self.cst_d, wr=[self.cst])
        self.dma("sp", self.norms.ap, self.norms_d, wr=[self.norms])
        self.op("dve", nc.vector.memset, dict(ap=self.ones_bf.ap, constant=1.0), wr=[self.ones_bf])
        self.op("dve", nc.vector.memset, dict(ap=self.ones_f.ap, constant=1.0), wr=[self.ones_f])
        self.op("dve", nc.vector.memset, dict(ap=self.epsb.ap, constant=EPS), wr=[self.epsb])
        self.ident = self.cst_t[:, 0:128]

        st = self.stages
        if st["l1"]:
            self.l1_setup()
        if st["l0"]:
            self.l0_setup()
        ntile = 2048 // TP
        if st["l0"]:
            for t in range(ntile):
                self.load_x(self.xp, t)
                self.l0_mixer(prefix=True, first_own=False)
        for t in range(ntile):
            self.load_x(self.xo, t)
            if st["l0"]:
                self.l0_mixer(prefix=False, first_own=(t == 0))
            if st["ffn0"]:
                self.ffn(0, 1)
            if st["l1"]:
                self.l1_mixer()
            if st["ffn1"]:
                self.ffn(1, 3)
            self.final(t, st["fnorm"])
        for i, sem in enumerate(self.dq["sp"]):
            n = self.dqn["sp"]
            cnt = (n // self.NS) + (1 if i < n % self.NS else 0)
            if cnt > 0:
                self._wait("sp", (sem, 16 * cnt, None))
        self.es.close()
        return nc

    def load_x(self, xd, t):
        nc = self.nc
        for blk in range(TP // 128):
            stg = self.xstg[0]
            r0 = t * TP + blk * 128
            self.dma("sp", stg.ap, xd[r0:r0 + 128, :], wr=[stg])
            for g4 in range(4):
                bk = self.bank()
                for j in range(4):
                    dt_ = g4 * 4 + j
                    self.op("pe", nc.tensor.transpose, dict(out=bk.ap[:, j * 128:(j + 1) * 128], in_=stg.ap[:, dt_ * 128:(dt_ + 1) * 128], identity=self.ident),
                            rd=[stg, self.cst], wr=[bk], mark=(j == 3))
                dst = self.hT_t[:, g4 * 4:(g4 + 1) * 4, blk * 128:(blk + 1) * 128]
                src = bk.ap.rearrange("p (a b) -> p a b", a=4)
                self.cp("act" if g4 % 2 == 0 else "dve", dst, src, rd=[bk], wr=self.hT[g4 * 4:(g4 + 1) * 4])

    def rstd_from_bank(self, bk, n, scale, P=128):
        self.act(self.rs.ap[0:P, 0:n], bk.ap[0:P, 0:n], AF.Sqrt, rd=[bk, self.epsb], wr=[self.rs], scale=scale, bias=self.eps_t[0:P, :])
        self.op("dve", self.nc.vector.reciprocal, dict(out=self.rs.ap[0:P, 0:n], in_=self.rs.ap[0:P, 0:n]), rd=[self.rs], wr=[self.rs])

    def rmsnorm(self, gi, outs):
        for hf in range(NH):
            sl = slice(hf * 512, (hf + 1) * 512)
            bk = self.bank()
            for dt_ in range(16):
                sq = self.sq[dt_ % 2]
                self.act(sq.ap, self.hT[dt_].ap[:, sl], AF.Square, rd=[self.hT[dt_]], wr=[sq])
                self.mm(bk.ap, self.ones_bf.ap, sq.ap, dt_ == 0, dt_ == 15, rd=[sq, self.ones_bf], wr=[bk], mark=True)
            self.rstd_from_bank(bk, 512, 1.0 / D)
            for dt_ in range(16):
                self.stt(outs[dt_].ap[:, sl], self.hT[dt_].ap[:, sl], self.norms_t[:, gi, dt_:dt_ + 1], self.rs.ap, ALU.mult, ALU.mult,
                         rd=[self.hT[dt_], self.rs, self.norms], wr=[outs[dt_]])

    def proj_fm(self, w, KT, rhs_tiles, hf, M=128, c0=0):
        bk = self.bank()
        w3 = w.ap.rearrange("p (k c) -> p k c", k=KT)
        for kt in range(KT):
            self.mm(bk.ap[0:M, :], w3[:, kt, c0:c0 + M], rhs_tiles[kt].ap[:, hf * 512:(hf + 1) * 512], kt == 0, kt == KT - 1,
                    rd=[w, rhs_tiles[kt]], wr=[bk], mark=(kt == KT - 1))
        return bk

    def outproj(self, wd, src):
        for db in range(16):
            w = self.wload(wd[db], 2048)
            for hf in range(NH):
                sl = slice(hf * 512, (hf + 1) * 512)
                bk = self.proj_fm(w, 16, src, hf)
                self.tt(self.hT[db].ap[:, sl], self.hT[db].ap[:, sl], bk.ap, ALU.add, rd=[bk, self.hT[db]], wr=[self.hT[db]])

    def ffn(self, layer, gi):
        self.rmsnorm(gi, self.hnT)
        for r in range(4):
            for f11 in range(11):
                fb = r * 11 + f11
                wg = self.wload(self.wg_d[layer, fb], 2048)
                wu = self.wload(self.wu_d[layer, fb], 2048)
                for hf in range(NH):
                    sl = slice(hf * 512, (hf + 1) * 512)
                    bg = self.proj_fm(wg, 16, self.hnT, hf)
                    bu = self.proj_fm(wu, 16, self.hnT, hf)
                    sg = self.sg[self.nsg % 2]
                    self.nsg += 1
                    self.act(sg.ap, bg.ap, AF.Silu, rd=[bg], wr=[sg])
                    self.tt(self.actT[f11].ap[:, sl], sg.ap, bu.ap, ALU.mult, rd=[sg, bu], wr=[self.actT[f11]])
            for db in range(16):
                w = self.wload(self.wd_d[layer, db][:, r * 1408:(r + 1) * 1408], 1408)
                for hf in range(NH):
                    sl = slice(hf * 512, (hf + 1) * 512)
                    bk = self.proj_fm(w, 11, self.actT, hf)
                    self.tt(self.hT[db].ap[:, sl], self.hT[db].ap[:, sl], bk.ap, ALU.add, rd=[bk, self.hT[db]], wr=[self.hT[db]])

    def final(self, t, do_norm):
        nc = self.nc
        if do_norm:
            self.rmsnorm(4, self.hT)
        for blk in range(TP // 128):
            stg = self.xstg[0]
            for g4 in range(4):
                bk = self.bank()
                for j in range(4):
                    dt_ = g4 * 4 + j
                    self.op("pe", nc.tensor.transpose, dict(out=bk.ap[:, j * 128:(j + 1) * 128], in_=self.hT[dt_].ap[:, blk * 128:(blk + 1) * 128], identity=self.ident),
                            rd=[self.hT[dt_], self.cst], wr=[bk], mark=(j == 3))
                self.cp("act" if g4 % 2 == 0 else "dve", stg.ap[:, g4 * 512:(g4 + 1) * 512], bk.ap, rd=[bk], wr=[stg])
            r0 = t * TP + blk * 128
            self.dma("sp", self.out_d[r0:r0 + 128, :], stg.ap, rd=[stg])

    def l1_setup(self):
        nc = self.nc
        self.lngb_t = self.sb("lngb", [128, 2, 16], F32)
        self.lngb = Buf(self.lngb_t[:])
        self.dma("sp", self.lngb.ap, self.lngb_d, wr=[self.lngb])
        self.wsf_t = self.hT_t[:, 0:2, :].rearrange("p a (b c) -> p (a b) c", c=128)
        self.wsf = self.hT[0]
        self.dma("sp", self.wsf_t, self.wsT_d, wr=[self.hT[0], self.hT[1]])
        self.bsb_t = self.hT_t[:, 2:4, :].rearrange("p a (b c) -> p (a b) c", c=128)
        self.bsb = self.hT[2]
        self.dma("sp", self.bsb_t, self.bs_d, wr=[self.hT[2], self.hT[3]])
        self.wsb_t = self.sb("wsb", [128, 8, 128], BF16)
        self.wsb = Buf(self.wsb_t[:])
        self.C2_t = self.sb("C2", [128, 16, 128], F32)
        self.C2 = Buf(self.C2_t[:])
        self.junk = self.xstg[0]
        self.st_t = self.sb("lnst", [128, 8], F32)
        self.st = Buf(self.st_t[:])
        for g in range(8):
            self.tt(self.wsf_t[:, g, :], self.wsf_t[:, g, :], self.cst_t[:, 704:832], ALU.mult, rd=[self.cst, self.hT[0], self.hT[1]], wr=[self.hT[0], self.hT[1]])
        self.cp("dve", self.wsb.ap, self.wsf_t, rd=[self.hT[0], self.hT[1]], wr=[self.wsb])
        for half in range(2):
            bk = self.bank()
            self.mm(bk.ap, self.ones_f.ap, self.wsf_t[:, half * 4:(half + 1) * 4, :], True, True, rd=[self.ones_f, self.hT[0], self.hT[1]], wr=[bk], mark=True)
            for gg in range(4):
                g = half * 4 + gg
                for cc in range(2):
                    ct = g * 2 + cc
                    self.stt(self.C2_t[:, ct, :], bk.ap[:, gg * 128:(gg + 1) * 128], self.lngb_t[:, 1, ct:ct + 1], self.bsb_t[:, g, :], ALU.mult, ALU.add,
                             rd=[bk, self.lngb, self.hT[2], self.hT[3]], wr=[self.C2])

    def l1_mixer(self):
        nc = self.nc
        self.rmsnorm(2, self.hnT)
        for ct in range(16):
            w = self.wload(self.oddu_d[ct], 2048)
            for hf in range(NH):
                bk = self.proj_fm(w, 16, self.hnT, hf)
                self.act(self.mixT[ct].ap[:, hf * 512:(hf + 1) * 512], bk.ap, AF.Gelu, rd=[bk], wr=[self.mixT[ct]])
        st = self.st_t
        for q2 in range(TP // 256):
            for cb in range(8):
                w = self.wload(self.oddv_d[cb], 4096)
                w3 = w.ap.rearrange("p (k c) -> p k c", k=16)
                for c2 in range(2):
                    ck = q2 * 2 + c2
                    bk = self.bank()
                    for kt in range(16):
                        self.mm(bk.ap[:, 0:256], self.hnT[kt].ap[:, ck * 128:(ck + 1) * 128], w3[:, kt, :], kt == 0, kt == 15, rd=[w, self.hnT[kt]], wr=[bk], mark=(kt == 15))
                    self.act(self.vg[c2].ap[:, cb * 256:(cb + 1) * 256], bk.ap[:, 0:256], AF.Gelu, rd=[bk], wr=[self.vg[c2]])
            for c2 in range(2):
                v = self.vg[c2]
                self.op("dve", nc.vector.tensor_reduce, dict(out=st[:, 0:1], in_=v.ap, axis=AX.X, op=ALU.add), rd=[v], wr=[self.st])
                self.op("dve", nc.vector.scalar_tensor_tensor, dict(out=self.junk.ap, in0=v.ap, scalar=1.0, in1=v.ap, op0=ALU.mult, op1=ALU.mult, accum_out=st[:, 1:2]),
                        rd=[v], wr=[self.junk, self.st])
                self.op("dve", nc.vector.tensor_scalar, dict(out=st[:, 2:3], in0=st[:, 0:1], scalar1=1.0 / 2048, scalar2=None, op0=ALU.mult), rd=[self.st], wr=[self.st], selfsync=True)
                self.tt(st[:, 3:4], st[:, 2:3], st[:, 2:3], ALU.mult, rd=[self.st], wr=[self.st])
                self.stt(st[:, 4:5], st[:, 1:2], 1.0 / 2048, st[:, 3:4], ALU.mult, ALU.subtract, rd=[self.st], wr=[self.st])
                self.act(st[:, 5:6], st[:, 4:5], AF.Sqrt, rd=[self.st, self.epsb], wr=[self.st], bias=self.eps_t[:, :])
                self.op("dve", nc.vector.reciprocal, dict(out=st[:, 6:7], in_=st[:, 5:6]), rd=[self.st], wr=[self.st])
                self.ts(v.ap, v.ap, st[:, 2:3], st[:, 6:7], ALU.subtract, ALU.mult, rd=[v, self.st], wr=[v])
            sl = slice(q2 * 256, (q2 + 1) * 256)
            for ct in range(16):
                bk = self.bank()
                for c2 in range(2):
                    self.mm(bk.ap[:, c2 * 128:(c2 + 1) * 128], self.vg[c2].ap[:, ct * 128:(ct + 1) * 128], self.wsb_t[:, ct // 2, :], True, True,
                            rd=[self.vg[c2], self.wsb], wr=[bk], mark=(c2 == 1))
                sg = self.sg[self.nsg % 2]
                self.nsg += 1
                for c2 in range(2):
                    cs = slice(c2 * 128, (c2 + 1) * 128)
                    self.stt(sg.ap[:, cs], bk.ap[:, cs], self.lngb_t[:, 0, ct:ct + 1], self.C2_t[:, ct, :], ALU.mult, ALU.add,
                             rd=[bk, self.lngb, self.C2], wr=[sg])
                self.tt(self.mixT[ct].ap[:, sl], self.mixT[ct].ap[:, sl], sg.ap[:, 0:256], ALU.mult, rd=[sg, self.mixT[ct]], wr=[self.mixT[ct]])
        self.outproj(self.wout1_d, self.mixT)

    def l0_setup(self):
        nc = self.nc
        f = F32
        self.hp_t = self.sb("hp", [128, 33], f)
        self.hp = Buf(self.hp_t[:])
        self.dma("sp", self.hp.ap, self.hp_d, wr=[self.hp])
        self.am0_t = self.sb("am0", [128, 128], f)
        self.am0 = Buf(self.am0_t[:])
        self.dma("sp", self.am0.ap, self.am0_d, wr=[self.am0])
        self.conv_t = self.sb("convw", [128, 24, 4], f)
        self.convw = Buf(self.conv_t[:])
        self.dma("sp", self.conv_t[:], self.conv_d, wr=[self.convw])
        self.am4_t = self.sb("am4", [128, 4, 2, 128], BF16)
        self.am4 = Buf(self.am4_t[:])
        self.am4f_t = self.sb("am4f", [128, 4, 2, 128], BF16)
        self.am4f = Buf(self.am4f_t[:])
        for i in range(4):
            self.cp("dve", self.am4_t[:, i, 0, :], self.cst_t[:, 576:704], rd=[self.cst], wr=[self.am4])
            self.cp("dve", self.am4_t[:, i, 1, :], self.cst_t[:, 704:832], rd=[self.cst], wr=[self.am4])
            self.cp("dve", self.am4f_t[:, i, 0, :], self.am0_t[:, :], rd=[self.am0], wr=[self.am4f])
            self.cp("dve", self.am4f_t[:, i, 1, :], self.cst_t[:, 704:832], rd=[self.cst], wr=[self.am4f])
        self.es_t = self.sb("es", [128, 16], f)
        self.esx = Buf(self.es_t[:])
        self.act(self.es_t[:], self.hp_t[:, 16:32], AF.Exp, rd=[self.hp], wr=[self.esx])
        nblk = TP // 128
        self.vtok_t = self.sb("vtok", [128, nblk + 1, 2, 128], BF16)
        self.vtok = Buf(self.vtok_t[:])
        self.op("dve", nc.vector.memset, dict(ap=self.vtok_t[:], constant=0.0), wr=[self.vtok])
        self.KT_t = self.sb("KT", [128, 2, 128 + TP], BF16)
        self.KT = [Buf(self.KT_t[:, i, :]) for i in range(2)]
        self.op("dve", nc.vector.memset, dict(ap=self.KT_t[:], constant=0.0), wr=self.KT)
        self.qTa = self.actT[0:8]
        self.P_t = self.sb("Pm", [128, 4, 2, 128], BF16)
        self.Pm = Buf(self.P_t[:])
        pass
        self.halo_t = self.sb("halo", [128, 24, 3], f)
        self.halo = Buf(self.halo_t[:])
        self.op("dve", nc.vector.memset, dict(ap=self.halo_t[:], constant=0.0), wr=[self.halo])
        self.S_t = self.sb("S", [128, 8, 128], f)
        self.S = [Buf(self.S_t[:, i, :]) for i in range(8)]
        self.op("dve", nc.vector.memset, dict(ap=self.S_t[:], constant=0.0), wr=self.S)
        self.nea_t = self.sb("nea", [128, 8], f)
        self.nea = Buf(self.nea_t[:])
        self.act(self.nea_t[:], self.hp_t[:, 0:8], AF.Exp, rd=[self.hp], wr=[self.nea])
        self.ts(self.nea_t[:], self.nea_t[:], -1.0, None, ALU.mult, None, rd=[self.nea], wr=[self.nea])
        names = ["U3n", "I3", "Sl3", "Su3", "mlo3", "mup3"]
        cols = [384, 512, 192, 448, 256, 320]
        self.c3 = {}
        for nm, c0 in zip(names, cols):
            self.c3[nm] = (self.cst_t[0:64, c0:c0 + 64].unsqueeze(1).to_broadcast([64, 8, 64]), self.cst)

        def mk(nm, shape, dt=f):
            t = self.sb(nm, shape, dt)
            return t, Buf(t[:])
        self.g_all = mk("g_all", [64, 8, 8])
        self.b_all = mk("b_all", [64, 8, 8])
        self.t88 = mk("t88", [64, 8, 8])
        self.cin = [mk("cin", [128, 515])] * 3
        self.cacc = mk("cacc", [128, 512])
        self.gqT = mk("gqT", [128, 512])
        self.gkT = mk("gkT", [128, 512])
        self.qs = self.gqT
        self.ks = self.gkT
        self.sqf = self.cacc
        self.gvT = mk("gvT", [128, 512])
        self.zs = mk("zs", [128, 512])
        self.g3 = mk("g3", [64, 8, 64])
        self.b3 = mk("b3", [64, 8, 64])
        self.rhs2 = mk("rhs2", [64, 8, 64])
        self.rhs3 = mk("rhs3", [64, 8, 64])
        self.ghc = mk("ghc", [64, 8])
        self.tA = self.g3
        self.tB = self.rhs3
        self.Dm = mk("Dm", [64, 8, 64])
        self.DmT = mk("DmT", [64, 8, 64])
        self.eg = mk("eg", [64, 8])
        self.egl = mk("egl", [64, 8])
        self.bg = mk("bg", [64, 8])
        self.glast = mk("glast", [128, 8])
        self.Mk = [mk("Mk%d" % i, [64, 8, 64]) for i in range(2)]
        self.Nk = [mk("Nk%d" % i, [64, 8, 64]) for i in range(2)]
        self.Rk = mk("Rk", [64, 8, 64])
        self.QKm = mk("QKm", [64, 8, 64])
        self.ktok = mk("ktok", [64, 8, 128])
        self.vtk = mk("vtk", [64, 8, 128])
        self.vb = self.vtk
        self.kbg = (self.vg_t[:, 1, :].bitcast(F32)[0:64, :].rearrange("p (a b) -> p a b", a=8), self.vg[1])
        self.kdec = self.ktok
        self.u_sb = (self.vg_t[:, 0, :].bitcast(F32)[0:64, :].rearrange("p (a b) -> p a b", a=8), self.vg[0])
        self.den = self.cacc[1]
        self.wT_sb = mk("wT_sb", [128, 512])
        self.qdT = mk("qdT", [128, 512])
        self.vnew = [mk("vnew%d" % i, [64, 128]) for i in range(2)]
        self.osb = (self.cin[0][0][:, 0:512], self.cin[0][1])

    def l0_mixer(self, prefix, first_own):
        nc = self.nc
        self.rmsnorm(0, self.hnT)
        nblk = TP // 128
        for kvh in range(2):
            w = self.wload(self.win_d[8 + kvh], 2048)
            for hf in range(NH):
                bk = self.proj_fm(w, 16, self.hnT, hf)
                self.cp("act", self.KT[kvh].ap[:, 128 + hf * 512:128 + (hf + 1) * 512], bk.ap, rd=[bk], wr=[self.KT[kvh]])
        wv = self.wload(self.win_d[10], 2048)
        wv3 = wv.ap.rearrange("p (k c) -> p k c", k=16)
        for blk in range(nblk):
            bk = self.bank()
            for kt in range(16):
                self.mm(bk.ap[:, 0:128], self.hnT[kt].ap[:, blk * 128:(blk + 1) * 128], wv3[:, kt, :], kt == 0, kt == 15, rd=[wv, self.hnT[kt]], wr=[bk], mark=(kt == 15))
            self.cp("dve", self.vtok_t[:, 1 + blk, :, 64:128], bk.ap[:, 0:128].rearrange("p (a b) -> p a b", a=2), rd=[bk], wr=[self.vtok])
        if not prefix:
            for ct in range(8):
                w = self.wload(self.win_d[ct], 2048)
                for hf in range(NH):
                    bk = self.proj_fm(w, 16, self.hnT, hf)
                    self.cp("act", self.qTa[ct].ap[:, hf * 512:(hf + 1) * 512], bk.ap, rd=[bk], wr=[self.qTa[ct]])
            self.attention(first_own)
        for kvh in range(2):
            self.cp("dve", self.KT[kvh].ap[:, 0:128], self.KT[kvh].ap[:, TP:TP + 128], rd=[self.KT[kvh]], wr=[self.KT[kvh]])
        self.cp("dve", self.vtok_t[:, 0, :, :], self.vtok_t[:, nblk, :, :], rd=[self.vtok], wr=[self.vtok])
        wba = self.wload(self.wba_d, 256)
        for hf in range(NH):
            self.gdn_pre(wba, hf)
            for h in range(8):
                self.gdn_head(h, hf, prefix)
        if not prefix:
            self.outproj(self.wout0_d, self.mixT)

    def attention(self, first_own):
        nc = self.nc
        Pt = self.P_t
        for blk in range(TP // 128):
            am = self.am4f if (first_own and blk == 0) else self.am4
            ts_ = slice(blk * 128, (blk + 1) * 128)
            for kvh in range(2):
                for par in range(2):
                    pr = slice(par * 64, par * 64 + 64)
                    b0 = self.bank()
                    b1 = self.bank()
                    for i in range(4):
                        ct = kvh * 4 + i
                        bk = b0 if i < 2 else b1
                        for j in range(2):
                            kcols = slice(blk * 128 + j * 128, blk * 128 + (j + 1) * 128)
                            oc = ((i % 2) * 2 + j) * 128
                            self.mm(bk.ap[:, oc:oc + 128], self.KT[kvh].ap[pr, kcols], self.qTa[ct].ap[pr, ts_], True, True,
                                    rd=[self.KT[kvh], self.qTa[ct]], wr=[bk], mark=(i % 2 == 1 and j == 1))
                    self.act(Pt[:, 0:2, :, :], b0.ap.rearrange("p (a b c) -> p a b c", a=2, b=2), AF.Exp, rd=[b0], wr=[self.Pm], scale=0.125)
                    self.act(Pt[:, 2:4, :, :], b1.ap.rearrange("p (a b c) -> p a b c", a=2, b=2), AF.Exp, rd=[b1], wr=[self.Pm], scale=0.125)
                    self.tt(Pt[:], Pt[:], am.ap, ALU.mult, rd=[self.Pm, am], wr=[self.Pm])
                    bo = self.bank()
                    bs = self.bank()
                    M = 64 if par == 0 else 128
                    for j in range(2):
                        vl = self.vtok_t[:, blk + j, kvh, 64:128] if par == 0 else self.vtok_t[:, blk + j, kvh, :]
                        self.mm(bo.ap[0:M, :], vl, Pt[:, :, j, :], j == 0, j == 1, rd=[self.vtok, self.Pm], wr=[bo], mark=(j == 1))
                    for j in range(2):
                        self.mm(bs.ap[0:M, :], self.ones_t[:, 0:M], Pt[:, :, j, :], j == 0, j == 1, rd=[self.ones_bf, self.Pm], wr=[bs], mark=(j == 1))
                    h0 = kvh * 8 + par
                    dn = self.den.ap[pr, :]
                    for i in range(4):
                        hh = h0 + 2 * i
                        self.ts(dn[:, i * 128:(i + 1) * 128], bs.ap[pr, i * 128:(i + 1) * 128], self.es_t[pr, hh:hh + 1], None, ALU.add, None, rd=[bs, self.esx], wr=[self.den])
                    self.op("dve", nc.vector.reciprocal, dict(out=dn, in_=dn), rd=[self.den], wr=[self.den])
                    dst = self.mix_t[pr, kvh * 4:(kvh + 1) * 4, ts_]
                    self.tt(dst, bo.ap[pr, :].rearrange("p (a b) -> p a b", a=4), dn.rearrange("p (a b) -> p a b", a=4), ALU.mult,
                            rd=[bo, self.den], wr=self.mixT[kvh * 4:(kvh + 1) * 4])

    def gdn_pre(self, wba, hf):
        nc = self.nc
        bk = self.banks[7]
        w3 = wba.ap.rearrange("p (k c) -> p k c", k=16)
        for c in range(8):
            t0 = hf * 512 + c * 64
            for kt in range(16):
                self.mm(bk.ap[0:64, c * 16:(c + 1) * 16], self.hnT[kt].ap[:, t0:t0 + 64], w3[:, kt, :], kt == 0, kt == 15,
                        rd=[wba, self.hnT[kt]], wr=[bk], mark=(kt == 15 and c == 7))
        raw = bk.ap[0:64, 0:128].rearrange("p (c k) -> p c k", c=8)
        self.act(self.b_all[0][:], raw[:, :, 0:8], AF.Sigmoid, rd=[bk], wr=[self.b_all[1]])
        dtb = self.hp_t[0:64, 8:16].unsqueeze(1).to_broadcast([64, 8, 8])
        self.tt(self.t88[0][:], raw[:, :, 8:16], dtb, ALU.add, rd=[bk, self.hp], wr=[self.t88[1]])
        self.act(self.t88[0][:], self.t88[0][:], AF.Exp, rd=[self.t88[1]], wr=[self.t88[1]])
        self.act(self.t88[0][:], self.t88[0][:], AF.Ln, rd=[self.t88[1], self.ones_f], wr=[self.t88[1]], bias=self.onesf_t[0:64, 0:1])
        nea = self.nea_t[0:64, :].unsqueeze(1).to_broadcast([64, 8, 8])
        self.tt(self.g_all[0][:], self.t88[0][:], nea, ALU.mult, rd=[self.t88[1], self.nea], wr=[self.g_all[1]])

    def gdn_head(self, h, hf, prefix):
        nc = self.nc
        B = self.banks
        sl = slice(hf * 512, (hf + 1) * 512)
        f64 = self.onesf_t[0:64, 0:64]
        f128 = self.onesf_t[0:64, 0:128]
        U64 = self.cst_t[0:64, 128:192]
        L64 = self.cst_t[0:64, 192:256]
        C = lambda t: t[0]
        outs = [self.qs, self.ks, self.gvT]
        for j in range(3 if True else 0):
            w = self.wload(self.win_d[11 + h * 4 + j], 2048)
            bk = self.proj_fm(w, 16, self.hnT, hf)
            ct = j * 8 + h
            cin_t, cin_b = self.cin[j]
            self.cp("act", cin_t[:, 3:515], bk.ap, rd=[bk], wr=[cin_b])
            self.cp("dve", cin_t[:, 0:3], self.halo_t[:, ct, :], rd=[self.halo], wr=[cin_b])
            acc_t, acc_b = self.cacc
            cw = self.conv_t
            self.ts(acc_t[:], cin_t[:, 3:515], cw[:, ct, 3:4], None, ALU.mult, None, rd=[cin_b, self.convw], wr=[acc_b])
            for kk in (2, 1, 0):
                self.stt(acc_t[:], cin_t[:, kk:kk + 512], cw[:, ct, kk:kk + 1], acc_t[:], ALU.mult, ALU.add, rd=[cin_b, self.convw, acc_b], wr=[acc_b])
            self.cp("dve", self.halo_t[:, ct, :], cin_t[:, 512:515], rd=[cin_b], wr=[self.halo])
            self.act(outs[j][0][:], acc_t[:], AF.Silu, rd=[acc_b], wr=[outs[j][1]])
        if not prefix:
            w = self.wload(self.win_d[11 + h * 4 + 3], 2048)
            bk = self.proj_fm(w, 16, self.hnT, hf)
            self.act(self.zs[0][:], bk.ap, AF.Silu, rd=[bk], wr=[self.zs[1]])
        for (src, dst, scl) in ((self.qs, self.gqT, 128 ** -0.5), (self.ks, self.gkT, 1.0)):
            self.act(self.sqf[0][:], src[0][:], AF.Square, rd=[src[1]], wr=[self.sqf[1]])
            bk = self.bank()
            self.mm(bk.ap, self.onesf_t[:, :], self.sqf[0][:], True, True, rd=[self.ones_f, self.sqf[1]], wr=[bk], mark=True)
            self.rstd_from_bank(bk, 512, 1.0)
            self.stt(dst[0][:], src[0][:], scl, self.rs.ap, ALU.mult, ALU.mult, rd=[src[1], self.rs], wr=[dst[1]])
        qT, kT, vT = self.gqT, self.gkT, self.gvT
        g_h = self.g_all[0][:, :, h]
        b_h = self.b_all[0][:, :, h]
        self.cp("dve", self.ghc[0][:], g_h, rd=[self.g_all[1]], wr=[self.ghc[1]])
        self.cp("dve", self.g3[0][:], g_h.unsqueeze(2).to_broadcast([64, 8, 64]), rd=[self.g_all[1]], wr=[self.g3[1]])
        self.cp("dve", self.b3[0][:], b_h.unsqueeze(2).to_broadcast([64, 8, 64]), rd=[self.b_all[1]], wr=[self.b3[1]])
        self.tt(self.rhs2[0][:], self.g3[0][:], self.c3["U3n"][0], ALU.mult, rd=[self.g3[1], self.c3["U3n"][1]], wr=[self.rhs2[1]])
        self.tt(self.rhs3[0][:], self.b3[0][:], self.c3["I3"][0], ALU.mult, rd=[self.b3[1], self.c3["I3"][1]], wr=[self.rhs3[1]])
        fl = lambda t: t[0][:].rearrange("p a b -> p (a b)")
        self.mm(B[0].ap[0:64, :], U64, fl(self.g3), True, False, rd=[self.cst, self.g3[1]], wr=[B[0]], mark=False)
        self.mm(B[0].ap[0:64, :], f64, fl(self.rhs2), False, True, rd=[self.ones_f, self.rhs2[1]], wr=[B[0]], mark=True)
        self.mm(B[1].ap[0:64, :], f64, fl(self.rhs3), True, True, rd=[self.ones_f, self.rhs3[1]], wr=[B[1]], mark=True)
        self.mm(B[2].ap[0:64, 0:8], U64, self.ghc[0][:], True, True, rd=[self.cst, self.ghc[1]], wr=[B[2]], mark=False)
        self.mm(B[2].ap[0:64, 8:16], L64, self.ghc[0][:], True, True, rd=[self.cst, self.ghc[1]], wr=[B[2]], mark=False)
        self.mm(B[2].ap[0:128, 16:24], f128, self.ghc[0][:], True, True, rd=[self.ones_f, self.ghc[1]], wr=[B[2]], mark=True)
        if not prefix:
            self.mm(B[3].ap[:, :], f128, fl(self.rhs2), True, True, rd=[self.ones_f, self.rhs2[1]], wr=[B[3]], mark=True)
        for c in range(8):
            cs = slice(c * 64, (c + 1) * 64)
            self.mm(B[4].ap[0:64, cs], kT[0][:, cs], kT[0][:, cs], True, True, rd=[kT[1]], wr=[B[4]], mark=(c == 7))
        if not prefix:
            for c in range(8):
                cs = slice(c * 64, (c + 1) * 64)
                self.mm(B[5].ap[0:64, cs], kT[0][:, cs], qT[0][:, cs], True, True, rd=[kT[1], qT[1]], wr=[B[5]], mark=(c == 7))
        v3 = lambda ap: ap.rearrange("p (a b) -> p a b", a=8)
        D3 = v3(B[0].ap[0:64, :])
        self.tt(self.tA[0][:], D3, self.c3["mlo3"][0], ALU.add, rd=[B[0], self.c3["mlo3"][1]], wr=[self.tA[1]])
        self.act(self.Dm[0][:], self.tA[0][:], AF.Exp, rd=[self.tA[1]], wr=[self.Dm[1]])
        self.stt(self.tB[0][:], D3, -1.0, self.c3["mup3"][0], ALU.mult, ALU.add, rd=[B[0], self.c3["mup3"][1]], wr=[self.tB[1]])
        self.act(self.DmT[0][:], self.tB[0][:], AF.Exp, rd=[self.tB[1]], wr=[self.DmT[1]])
        self.act(self.eg[0][:], B[2].ap[0:64, 0:8], AF.Exp, rd=[B[2]], wr=[self.eg[1]])
        self.act(self.egl[0][:], B[2].ap[0:64, 8:16], AF.Exp, rd=[B[2]], wr=[self.egl[1]])
        self.act(self.glast[0][:], B[2].ap[0:128, 16:24], AF.Exp, rd=[B[2]], wr=[self.glast[1]])
        if not prefix:
            self.act(self.qdT[0][:], B[3].ap[:, :], AF.Exp, rd=[B[3]], wr=[self.qdT[1]], scale=-1.0)
        KK3 = v3(B[4].ap[0:64, :])
        self.stt(self.tA[0][:], self.Dm[0][:], -1.0, self.c3["Sl3"][0], ALU.mult, ALU.mult, rd=[self.Dm[1], self.c3["Sl3"][1]], wr=[self.tA[1]])
        self.tt(self.tA[0][:], self.tA[0][:], self.b3[0][:], ALU.mult, rd=[self.tA[1], self.b3[1]], wr=[self.tA[1]])
        self.tt(self.Nk[0][0][:], KK3, self.tA[0][:], ALU.mult, rd=[B[4], self.tA[1]], wr=[self.Nk[0][1]])
        self.stt(self.tB[0][:], self.DmT[0][:], -1.0, self.c3["Su3"][0], ALU.mult, ALU.mult, rd=[self.DmT[1], self.c3["Su3"][1]], wr=[self.tB[1]])
        self.tt(self.tB[0][:], self.tB[0][:], v3(B[1].ap[0:64, :]), ALU.mult, rd=[self.tB[1], B[1]], wr=[self.tB[1]])
        self.tt(self.Mk[0][0][:], KK3, self.tB[0][:], ALU.mult, rd=[B[4], self.tB[1]], wr=[self.Mk[0][1]])
        if not prefix:
            self.tt(self.QKm[0][:], v3(B[5].ap[0:64, :]), self.DmT[0][:], ALU.mult, rd=[B[5], self.DmT[1]], wr=[self.QKm[1]])
        self.tt(self.Rk[0][:], self.Mk[0][0][:], self.c3["I3"][0], ALU.add, rd=[self.Mk[0][1], self.c3["I3"][1]], wr=[self.Rk[1]])
        for (src, dst, b0) in ((kT, self.ktok, 6), (vT, self.vtk, 0)):
            for c in range(8):
                bk = B[b0 + c // 4]
                self.op("pe", nc.tensor.transpose, dict(out=bk.ap[0:64, (c % 4) * 128:(c % 4 + 1) * 128], in_=src[0][:, c * 64:(c + 1) * 64], identity=self.ident),
                        rd=[src[1], self.cst], wr=[bk], mark=(c % 4 == 3))
            for hh in range(2):
                self.cp("act", dst[0][:, hh * 4:(hh + 1) * 4, :], B[b0 + hh].ap[0:64, :].rearrange("p (a b) -> p a b", a=4), rd=[B[b0 + hh]], wr=[dst[1]])
        cur = 0
        for lvl in range(1, 6):
            nxt = 1 - cur
            Mc, Nc = self.Mk[cur], self.Nk[cur]
            Mn, Nn = self.Mk[nxt], self.Nk[nxt]
            if lvl < 5:
                for c in range(8):
                    cs = slice(c * 64, (c + 1) * 64)
                    self.mm(B[2].ap[0:64, cs], Nc[0][:, c, :], Mc[0][:, c, :], True, True, rd=[Nc[1], Mc[1]], wr=[B[2]], mark=(c == 7))
            for c in range(8):
                cs = slice(c * 64, (c + 1) * 64)
                self.mm(B[3].ap[0:64, cs], Mc[0][:, c, :], Nc[0][:, c, :], True, True, rd=[Nc[1], Mc[1]], wr=[B[3]], mark=(c == 7))
            if lvl < 5:
                self.cp("act", Mn[0][:], v3(B[2].ap[0:64, :]), rd=[B[2]], wr=[Mn[1]])
            self.cp("dve", Nn[0][:], v3(B[3].ap[0:64, :]), rd=[B[3]], wr=[Nn[1]])
            for c in range(8):
                cs = slice(c * 64, (c + 1) * 64)
                self.mm(B[4].ap[0:64, cs], Nn[0][:, c, :], self.Rk[0][:, c, :], True, True, rd=[Nn[1], self.Rk[1]], wr=[B[4]], mark=(c == 7))
            self.tt(self.Rk[0][:], self.Rk[0][:], v3(B[4].ap[0:64, :]), ALU.add, rd=[B[4], self.Rk[1]], wr=[self.Rk[1]])
            cur = nxt
        self.tt(self.bg[0][:], b_h, self.eg[0][:], ALU.mult, rd=[self.b_all[1], self.eg[1]], wr=[self.bg[1]])
        self.tt(self.vb[0][:], self.vtk[0][:], b_h.unsqueeze(2).to_broadcast([64, 8, 128]), ALU.mult, rd=[self.vtk[1], self.b_all[1]], wr=[self.vb[1]])
        self.tt(self.kbg[0][:], self.ktok[0][:], self.bg[0][:].unsqueeze(2).to_broadcast([64, 8, 128]), ALU.mult, rd=[self.ktok[1], self.bg[1]], wr=[self.kbg[1]])
        self.tt(self.kdec[0][:], self.ktok[0][:], self.egl[0][:].unsqueeze(2).to_broadcast([64, 8, 128]), ALU.mult, rd=[self.ktok[1], self.egl[1]], wr=[self.kdec[1]])
        for c in range(8):
            bk = B[5 + c // 4]
            self.mm(bk.ap[0:64, (c % 4) * 128:(c % 4 + 1) * 128], self.Rk[0][:, c, :], self.vb[0][:, c, :], True, True, rd=[self.Rk[1], self.vb[1]], wr=[bk], mark=(c % 4 == 3))
        for hh in range(2):
            self.cp("act", self.u_sb[0][:, hh * 4:(hh + 1) * 4, :], B[5 + hh].ap[0:64, :].rearrange("p (a b) -> p a b", a=4), rd=[B[5 + hh]], wr=[self.u_sb[1]])
        for c in range(8):
            cs = slice(c * 64, (c + 1) * 64)
            self.mm(B[7].ap[:, cs], self.kbg[0][:, c, :], self.Rk[0][:, c, :], True, True, rd=[self.kbg[1], self.Rk[1]], wr=[B[7]], mark=(c == 7))
        self.cp("dve", self.wT_sb[0][:], B[7].ap[:, :], rd=[B[7]], wr=[self.wT_sb[1]])
        if not prefix:
            self.tt(self.qdT[0][:], qT[0][:], self.qdT[0][:], ALU.mult, rd=[qT[1], self.qdT[1]], wr=[self.qdT[1]])
        S = self.S[h]
        for c in range(8):
            cs = slice(c * 64, (c + 1) * 64)
            vb_ = B[1 + c % 2]
            self.mm(vb_.ap[0:64, 0:128], self.wT_sb[0][:, cs], S.ap, True, True, rd=[self.wT_sb[1], S], wr=[vb_], mark=True)
            vn = self.vnew[c % 2]
            self.tt(vn[0][:], self.u_sb[0][:, c, :], vb_.ap[0:64, 0:128], ALU.subtract, rd=[self.u_sb[1], vb_], wr=[vn[1]])
            if not prefix:
                self.mm(B[0].ap[:, cs], S.ap, self.qdT[0][:, cs], True, False, rd=[S, self.qdT[1]], wr=[B[0]], mark=False)
                self.mm(B[0].ap[:, cs], vn[0][:], self.QKm[0][:, c, :], False, True, rd=[vn[1], self.QKm[1]], wr=[B[0]], mark=False)
            sb_ = B[3 + c % 2]
            self.mm(sb_.ap[:, 0:128], self.kdec[0][:, c, :], vn[0][:], True, True, rd=[self.kdec[1], vn[1]], wr=[sb_], mark=True)
            self.stt(S.ap, S.ap, self.glast[0][:, c:c + 1], sb_.ap[:, 0:128], ALU.mult, ALU.add, rd=[S, self.glast[1], sb_], wr=[S])
        if not prefix:
            self.cp("act", self.osb[0][:], B[0].ap[:, :], rd=[B[0]], wr=[self.osb[1]])
            self.act(self.sqf[0][:], self.osb[0][:], AF.Square, rd=[self.osb[1]], wr=[self.sqf[1]])
            bk = B[5]
            self.mm(bk.ap, self.onesf_t[:, :], self.sqf[0][:], True, True, rd=[self.ones_f, self.sqf[1]], wr=[bk], mark=True)
            self.rstd_from_bank(bk, 512, 1.0 / 128)
            self.stt(self.osb[0][:], self.osb[0][:], self.hp_t[:, 32:33], self.rs.ap, ALU.mult, ALU.mult, rd=[self.osb[1], self.hp, self.rs], wr=[self.osb[1]])
            self.tt(self.mixT[8 + h].ap[:, sl], self.osb[0][:], self.zs[0][:], ALU.mult, rd=[self.osb[1], self.zs[1]], wr=[self.mixT[8 + h]])


def blockify(Wc):
    K, C = Wc.shape
    return np.ascontiguousarray(Wc.reshape(K // 128, 128, C).transpose(1, 0, 2)).reshape(128, (K // 128) * C)


def col16(v):
    return np.ascontiguousarray(v.reshape(16, 128).T)


def make_consts():
    c = np.zeros((128, NCST), np.float32)
    c[:, 0:128] = np.eye(128, dtype=np.float32)
    i = np.arange(64)
    c[0:64, 128:192] = (i[:, None] <= i[None, :])
    c[0:64, 192:256] = (i[:, None] > i[None, :])
    c[0:64, 256:320] = np.where(i[:, None] >= i[None, :], 0.0, -1e4)
    c[0:64, 320:384] = np.where(i[None, :] >= i[:, None], 0.0, -1e4)
    c[0:64, 384:448] = -(i[:, None] <= i[None, :]).astype(np.float32)
    c[0:64, 448:512] = (i[None, :] > i[:, None])
    c[0:64, 512:576] = np.eye(64)
    k = np.arange(128)
    c[:, 576:704] = (k[:, None] > k[None, :])
    c[:, 704:832] = (k[:, None] <= k[None, :])
    return c


STAGES = dict(l0=True, ffn0=True, l1=True, ffn1=True, fnorm=True)
_NC_CACHE = {}


def prep_shared(inp):
    f = np.float32
    sh = {}
    sh["cst"] = make_consts()
    norms = np.zeros((128, 6, 16), f)
    norms[:, 0] = col16(inp["even_norm"][0])
    norms[:, 1] = col16(inp["ffn_norm"][0])
    norms[:, 2] = col16(inp["odd_norm"][0])
    norms[:, 3] = col16(inp["ffn_norm"][1])
    norms[:, 4] = col16(inp["final_norm"])
    sh["norms"] = norms
    wi = inp["even_w_in"][0]
    blocks = []
    for ct in range(8):
        blocks.append(blockify(wi[:, ct * 128:(ct + 1) * 128]))
    for kvh in range(2):
        kc = wi[:, 1024 + kvh * 64:1024 + (kvh + 1) * 64]
        blocks.append(blockify(np.concatenate([kc, kc], axis=1)))
    blocks.append(blockify(wi[:, 1152:1280]))
    for h in range(8):
        for base in (1280, 2304, 3328, 4352):
            blocks.append(blockify(wi[:, base + h * 128:base + (h + 1) * 128]))
    sh["win"] = np.stack(blocks)
    sh["wba"] = blockify(wi[:, 5376:5392])
    cv = inp["even_conv"][0]
    sh["conv"] = np.ascontiguousarray(cv.T.reshape(24, 128, 4).transpose(1, 0, 2))
    hp = np.zeros((128, 33), f)
    hp[:, 0:8] = inp["even_a_log"][0][None, :]
    hp[:, 8:16] = inp["even_dt_bias"][0][None, :]
    hp[:, 16:32] = inp["even_sinks"][0][None, :]
    hp[:, 32] = inp["even_onorm"][0]
    sh["hp"] = hp
    wo = inp["even_w_out"][0]
    sh["wout0"] = np.stack([blockify(wo[:, d * 128:(d + 1) * 128]) for d in range(16)])
    ow = inp["odd_w_in"][0]
    sh["oddu"] = np.stack([blockify(ow[:, c * 128:(c + 1) * 128]) for c in range(16)])
    sh["oddv"] = np.stack([blockify(ow[:, 2048 + c * 256:2048 + (c + 1) * 256]) for c in range(8)])
    lngb = np.zeros((128, 2, 16), f)
    lngb[:, 0] = col16(inp["odd_ln_g"][0])
    lngb[:, 1] = col16(inp["odd_ln_b"][0])
    sh["lngb"] = lngb
    sh["wsT"] = np.ascontiguousarray(inp["odd_w_s"][0].transpose(2, 0, 1))
    sh["bsb"] = np.ascontiguousarray(np.broadcast_to(inp["odd_b_s"][0][None], (128, 8, 128)))
    wo1 = inp["odd_w_out"][0]
    sh["wout1"] = np.stack([blockify(wo1[:, d * 128:(d + 1) * 128]) for d in range(16)])
    sh["wg"] = np.stack([np.stack([blockify(inp["ffn_w_gate"][l][:, c * 128:(c + 1) * 128]) for c in range(44)]) for l in range(2)])
    sh["wu"] = np.stack([np.stack([blockify(inp["ffn_w_up"][l][:, c * 128:(c + 1) * 128]) for c in range(44)]) for l in range(2)])
    sh["wd"] = np.stack([np.stack([blockify(inp["ffn_w_down"][l][:, c * 128:(c + 1) * 128]) for c in range(16)]) for l in range(2)])
    return sh


def kernel(**inp):
    inp = {k: np.asarray(v) for k, v in inp.items()}
    key = tuple(sorted(STAGES.items()))
    if key not in _NC_CACHE:
        _NC_CACHE[key] = KB(dict(STAGES)).build()
    nc = _NC_CACHE[key]
    sh = prep_shared(inp)
    x = inp["x"]
    k = np.arange(128)
    su = (k[:, None] > k[None, :]).astype(np.float32)
    in_maps = []
    for c in range(8):
        b, half = c // 2, c % 2
        m = dict(sh)
        m["xo"] = np.ascontiguousarray(x[b, half * 2048:(half + 1) * 2048])
        m["xp"] = np.ascontiguousarray(x[b, 0:2048]) if half == 1 else np.zeros((2048, D), np.float32)
        m["am0"] = su if half == 1 else np.zeros((128, 128), np.float32)
        in_maps.append(m)
    import os
    res = run_bass_kernel_spmd(nc, in_maps, core_ids=list(range(8)))
    out = np.zeros((4, 4096, D), np.float32)
    for c in range(8):
        b, half = c // 2, c % 2
        out[b, half * 2048:(half + 1) * 2048] = res.results[c]["out"]
    return out
```

```python
import numpy as np
from contextlib import ExitStack
import concourse.bass as bass
import concourse.mybir as mybir
from concourse.bass_utils import run_bass_kernel_spmd

F32 = mybir.dt.float32
BF16 = mybir.dt.bfloat16
AF = mybir.ActivationFunctionType
ALU = mybir.AluOpType
AX = mybir.AxisListType

D = 2048
DFF = 5632
NFT = 44
EPS = 1e-6
TP = 512
NH = TP // 512
NCST = 832
import os as _os
SELFSYNC_ALL = _os.environ.get('KSELFSYNC', '0') == '1'


class Buf:
    __slots__ = ("ap", "wev", "revs")

    def __init__(self, ap):
        self.ap = ap
        self.wev = None
        self.revs = {}


class KB:
    def __init__(self, stages):
        self.stages = stages
        self.nc = bass.Bass("TRN2", target_bir_lowering=False)
        self.es = ExitStack()
        nc = self.nc
        self.engs = {"pe": nc.tensor, "act": nc.scalar, "dve": nc.vector, "pool": nc.gpsimd, "sp": nc.sync}
        self.psem = {e: self.es.enter_context(nc.semaphore("p_" + e)) for e in ("pe", "act", "dve")}
        self.pcnt = {e: 0 for e in self.psem}
        self.pending = {e: ([], []) for e in self.psem}
        self.waited = {}
        self.NS = 12
        self.dq = {q: [self.es.enter_context(nc.semaphore("d_%s_%d" % (q, i))) for i in range(self.NS)] for q in ("sp", "pool")}
        self.dqn = {"sp": 0, "pool": 0}
        self.nbank = 0

    def sb(self, name, shape, dt):
        return self.es.enter_context(self.nc.sbuf_tensor("s_" + name, list(shape), dt))

    def dram(self, name, shape, dt=F32, kind="ExternalInput"):
        return self.nc.dram_tensor(name, list(shape), dt, kind=kind).ap()

    def _wait(self, eng, ev):
        if ev is None:
            return
        sem, val, src = ev
        if src == eng and eng == "pe":
            return
        key = (eng, sem.name if hasattr(sem, "name") else id(sem))
        if self.waited.get(key, 0) >= val:
            return
        self.engs[eng].wait_ge(sem, val)
        self.waited[key] = val

    def op(self, eng, fn, kw, rd=(), wr=(), mark=True, selfsync=False):
        if (selfsync or (SELFSYNC_ALL and eng in ("act", "dve"))) and self.pcnt[eng] > 0:
            self.engs[eng].wait_ge(self.psem[eng], self.pcnt[eng])
        for b in rd:
            self._wait(eng, b.wev)
        for b in wr:
            self._wait(eng, b.wev)
            for e in list(b.revs.values()):
                self._wait(eng, e)
        inst = fn(**kw)
        pr, pw = self.pending[eng]
        pr.extend(rd)
        pw.extend(wr)
        if mark:
            self.pcnt[eng] += 1
            inst.then_inc(self.psem[eng], 1)
            ev = (self.psem[eng], self.pcnt[eng], eng)
            for b in pr:
                b.revs[eng] = ev
            for b in pw:
                b.wev = ev
                b.revs = {}
            self.pending[eng] = ([], [])
        return inst

    def dma(self, q, out_ap, in_ap, rd=(), wr=()):
        for b in rd:
            self._wait(q, b.wev)
        for b in wr:
            self._wait(q, b.wev)
            for e in list(b.revs.values()):
                self._wait(q, e)
        n = self.dqn[q]
        sem = self.dq[q][n % self.NS]
        val = 16 * (n // self.NS + 1)
        if val > 16:
            self._wait(q, (sem, val - 16, None))
        inst = self.engs[q].dma_start(out=out_ap, in_=in_ap)
        inst.then_inc(sem, 16)
        ev = (sem, val, None)
        for b in rd:
            b.revs[("dma", q, n % self.NS)] = ev
        for b in wr:
            b.wev = ev
            b.revs = {}
        self.dqn[q] = n + 1
        return ev

    def bank(self):
        b = self.banks[self.nbank % 8]
        self.nbank += 1
        return b

    def mm(self, out, lhsT, rhs, start, stop, rd, wr, mark):
        return self.op("pe", self.nc.tensor.matmul, dict(out=out, lhsT=lhsT, rhs=rhs, start=start, stop=stop), rd=rd, wr=wr, mark=mark)

    def act(self, out, in_, func, rd, wr, **kw):
        return self.op("act", self.nc.scalar.activation, dict(out=out, in_=in_, func=func, **kw), rd=rd, wr=wr)

    def tt(self, out, in0, in1, op, rd, wr):
        return self.op("dve", self.nc.vector.tensor_tensor, dict(out=out, in0=in0, in1=in1, op=op), rd=rd, wr=wr)

    def ts(self, out, in0, s1, s2, op0, op1, rd, wr):
        kw = dict(out=out, in0=in0, scalar1=s1, scalar2=s2, op0=op0)
        if op1 is not None:
            kw["op1"] = op1
        return self.op("dve", self.nc.vector.tensor_scalar, kw, rd=rd, wr=wr)

    def stt(self, out, in0, scalar, in1, op0, op1, rd, wr):
        return self.op("dve", self.nc.vector.scalar_tensor_tensor, dict(out=out, in0=in0, scalar=scalar, in1=in1, op0=op0, op1=op1), rd=rd, wr=wr)

    def cp(self, eng, out, in_, rd, wr):
        if eng == "act":
            return self.act(out, in_, AF.Copy, rd, wr)
        return self.op("dve", self.nc.vector.tensor_copy, dict(out=out, in_=in_), rd=rd, wr=wr)

    def wload(self, dram_ap, n):
        A = self.W_N
        if self.wptr + n > A:
            self.wptr = 0
        s, e = self.wptr, self.wptr + n
        self.wptr = e
        deps = [b for (a, bn, b) in self.wlive if a < e and bn > s]
        self.wlive = [(a, bn, b) for (a, bn, b) in self.wlive if not (a < e and bn > s)]
        buf = Buf(self.warena[:, s:e])
        self.dma("pool", buf.ap, dram_ap, wr=deps + [buf])
        self.wlive.append((s, e, buf))
        return buf

    def build(self):
        nc = self.nc
        kb = self
        self.xo = self.dram("xo", [2048, D])
        self.xp = self.dram("xp", [2048, D])
        self.cst_d = self.dram("cst", [128, NCST])
        self.am0_d = self.dram("am0", [128, 128])
        self.norms_d = self.dram("norms", [128, 6, 16])
        self.win_d = self.dram("win", [43, 128, 2048])
        self.wba_d = self.dram("wba", [128, 256])
        self.conv_d = self.dram("conv", [128, 24, 4])
        self.hp_d = self.dram("hp", [128, 8 + 8 + 16 + 1])
        self.wout0_d = self.dram("wout0", [16, 128, 2048])
        self.oddu_d = self.dram("oddu", [16, 128, 2048])
        self.oddv_d = self.dram("oddv", [8, 128, 4096])
        self.lngb_d = self.dram("lngb", [128, 2, 16])
        self.wsT_d = self.dram("wsT", [128, 8, 128])
        self.bs_d = self.dram("bsb", [128, 8, 128])
        self.wout1_d = self.dram("wout1", [16, 128, 2048])
        self.wg_d = self.dram("wg", [2, 44, 128, 2048])
        self.wu_d = self.dram("wu", [2, 44, 128, 2048])
        self.wd_d = self.dram("wd", [2, 16, 128, 5632])
        self.out_d = self.dram("out", [2048, D], kind="ExternalOutput")

        self.hT_t = self.sb("hT", [128, 16, TP], F32)
        self.hT = [Buf(self.hT_t[:, i, :]) for i in range(16)]
        self.hn_t = self.sb("hnT", [128, 16, TP], BF16)
        self.hnT = [Buf(self.hn_t[:, i, :]) for i in range(16)]
        self.mix_t = self.sb("mixT", [128, 16, TP], BF16)
        self.mixT = [Buf(self.mix_t[:, i, :]) for i in range(16)]
        self.act_t = self.sb("actT", [128, 11, TP], BF16)
        self.actT = [Buf(self.act_t[:, i, :]) for i in range(11)]
        self.W_N = 10240
        self.warena = self.sb("warena", [128, self.W_N], BF16)
        self.wptr = 0
        self.wlive = []
        self.cst_t = self.sb("cst", [128, NCST], F32)
        self.cst = Buf(self.cst_t[:])
        self.norms_t = self.sb("norms", [128, 6, 16], F32)
        self.norms = Buf(self.norms_t[:])
        self.ones_t = self.sb("ones_bf", [128, 128], BF16)
        self.ones_bf = Buf(self.ones_t[:])
        self.onesf_t = self.sb("ones_f", [128, 128], F32)
        self.ones_f = Buf(self.onesf_t[:])
        self.eps_t = self.sb("eps", [128, 1], F32)
        self.epsb = Buf(self.eps_t[:])
        self.sq = [Buf(self.sb("sq%d" % i, [128, 512], BF16)[:]) for i in range(2)]
        self.rs = Buf(self.sb("rs", [128, 512], F32)[:])
        self.sg = [Buf(self.sb("sg%d" % i, [128, 512], F32)[:]) for i in range(2)]
        self.xstg = [Buf(self.sb("xstg%d" % i, [128, D], F32)[:]) for i in range(1)]
        self.nsg = 0
        self.vg_t = self.sb("vg", [128, 2, 2048], BF16)
        self.vg = [Buf(self.vg_t[:, i, :]) for i in range(2)]
        ps = [self.es.enter_context(nc.psum_tensor("ps%d" % i, [128, 512], F32)) for i in range(8)]
        self.banks = [Buf(p[:]) for p in ps]

        self.dma("sp", self.cst.ap, self.cst_d, wr=[self.cst])
        self.dma("sp", self.norms.ap, self.norms_d, wr=[self.norms])
        self.op("dve", nc.vector.memset, dict(ap=self.ones_bf.ap, constant=1.0), wr=[self.ones_bf])
        self.op("dve", nc.vector.memset, dict(ap=self.ones_f.ap, constant=1.0), wr=[self.ones_f])
        self.op("dve", nc.vector.memset, dict(ap=self.epsb.ap, constant=EPS), wr=[self.epsb])
        self.ident = self.cst_t[:, 0:128]

        st = self.stages
        if st["l1"]:
            self.l1_setup()
        if st["l0"]:
            self.l0_setup()
        ntile = 2048 // TP
        if st["l0"]:
            for t in range(ntile):
                self.load_x(self.xp, t)
                self.l0_mixer(prefix=True, first_own=False, lastp=(t == ntile - 1))
        for t in range(ntile):
            self.load_x(self.xo, t)
            if st["l0"]:
                self.l0_mixer(prefix=False, first_own=(t == 0))
            if st["ffn0"]:
                self.ffn(0, 1)
            if st["l1"]:
                self.l1_mixer()
            if st["ffn1"]:
                self.ffn(1, 3)
            self.final(t, st["fnorm"])
        for i, sem in enumerate(self.dq["sp"]):
            n = self.dqn["sp"]
            cnt = (n // self.NS) + (1 if i < n % self.NS else 0)
            if cnt > 0:
                self._wait("sp", (sem, 16 * cnt, None))
        self.es.close()
        return nc

    def load_x(self, xd, t):
        nc = self.nc
        for blk in range(TP // 128):
            stg = self.xstg[0]
            r0 = t * TP + blk * 128
            self.dma("sp", stg.ap, xd[r0:r0 + 128, :], wr=[stg])
            for g4 in range(4):
                bk = self.bank()
                for j in range(4):
                    dt_ = g4 * 4 + j
                    self.op("pe", nc.tensor.transpose, dict(out=bk.ap[:, j * 128:(j + 1) * 128], in_=stg.ap[:, dt_ * 128:(dt_ + 1) * 128], identity=self.ident),
                            rd=[stg, self.cst], wr=[bk], mark=(j == 3))
                dst = self.hT_t[:, g4 * 4:(g4 + 1) * 4, blk * 128:(blk + 1) * 128]
                src = bk.ap.rearrange("p (a b) -> p a b", a=4)
                self.cp("act" if g4 % 2 == 0 else "dve", dst, src, rd=[bk], wr=self.hT[g4 * 4:(g4 + 1) * 4])

    def rstd_from_bank(self, bk, n, scale, P=128):
        self.act(self.rs.ap[0:P, 0:n], bk.ap[0:P, 0:n], AF.Sqrt, rd=[bk, self.epsb], wr=[self.rs], scale=scale, bias=self.eps_t[0:P, :])
        self.op("dve", self.nc.vector.reciprocal, dict(out=self.rs.ap[0:P, 0:n], in_=self.rs.ap[0:P, 0:n]), rd=[self.rs], wr=[self.rs])

    def rmsnorm(self, gi, outs):
        for hf in range(NH):
            sl = slice(hf * 512, (hf + 1) * 512)
            bk = self.bank()
            for dt_ in range(16):
                sq = self.sq[dt_ % 2]
                self.act(sq.ap, self.hT[dt_].ap[:, sl], AF.Square, rd=[self.hT[dt_]], wr=[sq])
                self.mm(bk.ap, self.ones_bf.ap, sq.ap, dt_ == 0, dt_ == 15, rd=[sq, self.ones_bf], wr=[bk], mark=True)
            self.rstd_from_bank(bk, 512, 1.0 / D)
            for dt_ in range(16):
                self.stt(outs[dt_].ap[:, sl], self.hT[dt_].ap[:, sl], self.norms_t[:, gi, dt_:dt_ + 1], self.rs.ap, ALU.mult, ALU.mult,
                         rd=[self.hT[dt_], self.rs, self.norms], wr=[outs[dt_]])

    def proj_fm(self, w, KT, rhs_tiles, hf, M=128, c0=0):
        bk = self.bank()
        w3 = w.ap.rearrange("p (k c) -> p k c", k=KT)
        for kt in range(KT):
            self.mm(bk.ap[0:M, :], w3[:, kt, c0:c0 + M], rhs_tiles[kt].ap[:, hf * 512:(hf + 1) * 512], kt == 0, kt == KT - 1,
                    rd=[w, rhs_tiles[kt]], wr=[bk], mark=(kt == KT - 1))
        return bk

    def outproj(self, wd, src):
        for db in range(16):
            w = self.wload(wd[db], 2048)
            for hf in range(NH):
                sl = slice(hf * 512, (hf + 1) * 512)
                bk = self.proj_fm(w, 16, src, hf)
                self.tt(self.hT[db].ap[:, sl], self.hT[db].ap[:, sl], bk.ap, ALU.add, rd=[bk, self.hT[db]], wr=[self.hT[db]])

    def ffn(self, layer, gi):
        self.rmsnorm(gi, self.hnT)
        for r in range(4):
            for f11 in range(11):
                fb = r * 11 + f11
                wg = self.wload(self.wg_d[layer, fb], 2048)
                wu = self.wload(self.wu_d[layer, fb], 2048)
                for hf in range(NH):
                    sl = slice(hf * 512, (hf + 1) * 512)
                    bg = self.proj_fm(wg, 16, self.hnT, hf)
                    bu = self.proj_fm(wu, 16, self.hnT, hf)
                    sg = self.sg[self.nsg % 2]
                    self.nsg += 1
                    self.act(sg.ap, bg.ap, AF.Silu, rd=[bg], wr=[sg])
                    self.tt(self.actT[f11].ap[:, sl], sg.ap, bu.ap, ALU.mult, rd=[sg, bu], wr=[self.actT[f11]])
            for db in range(16):
                w = self.wload(self.wd_d[layer, db][:, r * 1408:(r + 1) * 1408], 1408)
                for hf in range(NH):
                    sl = slice(hf * 512, (hf + 1) * 512)
                    bk = self.proj_fm(w, 11, self.actT, hf)
                    self.tt(self.hT[db].ap[:, sl], self.hT[db].ap[:, sl], bk.ap, ALU.add, rd=[bk, self.hT[db]], wr=[self.hT[db]])

    def final(self, t, do_norm):
        nc = self.nc
        if do_norm:
            self.rmsnorm(4, self.hT)
        for blk in range(TP // 128):
            stg = self.xstg[0]
            for g4 in range(4):
                bk = self.bank()
                for j in range(4):
                    dt_ = g4 * 4 + j
                    self.op("pe", nc.tensor.transpose, dict(out=bk.ap[:, j * 128:(j + 1) * 128], in_=self.hT[dt_].ap[:, blk * 128:(blk + 1) * 128], identity=self.ident),
                            rd=[self.hT[dt_], self.cst], wr=[bk], mark=(j == 3))
                self.cp("act" if g4 % 2 == 0 else "dve", stg.ap[:, g4 * 512:(g4 + 1) * 512], bk.ap, rd=[bk], wr=[stg])
            r0 = t * TP + blk * 128
            self.dma("sp", self.out_d[r0:r0 + 128, :], stg.ap, rd=[stg])

    def l1_setup(self):
        nc = self.nc
        self.lngb_t = self.sb("lngb", [128, 2, 16], F32)
        self.lngb = Buf(self.lngb_t[:])
        self.dma("sp", self.lngb.ap, self.lngb_d, wr=[self.lngb])
        self.wsf_t = self.hT_t[:, 0:2, :].rearrange("p a (b c) -> p (a b) c", c=128)
        self.wsf = self.hT[0]
        self.dma("sp", self.wsf_t, self.wsT_d, wr=[self.hT[0], self.hT[1]])
        self.bsb_t = self.hT_t[:, 2:4, :].rearrange("p a (b c) -> p (a b) c", c=128)
        self.bsb = self.hT[2]
        self.dma("sp", self.bsb_t, self.bs_d, wr=[self.hT[2], self.hT[3]])
        self.wsb_t = self.sb("wsb", [128, 8, 128], BF16)
        self.wsb = Buf(self.wsb_t[:])
        self.C2_t = self.sb("C2", [128, 16, 128], F32)
        self.C2 = Buf(self.C2_t[:])
        self.junk = self.xstg[0]
        self.st_t = self.sb("lnst", [128, 8], F32)
        self.st = Buf(self.st_t[:])
        for g in range(8):
            self.tt(self.wsf_t[:, g, :], self.wsf_t[:, g, :], self.cst_t[:, 704:832], ALU.mult, rd=[self.cst, self.hT[0], self.hT[1]], wr=[self.hT[0], self.hT[1]])
        self.cp("dve", self.wsb.ap, self.wsf_t, rd=[self.hT[0], self.hT[1]], wr=[self.wsb])
        for half in range(2):
            bk = self.bank()
            self.mm(bk.ap, self.ones_f.ap, self.wsf_t[:, half * 4:(half + 1) * 4, :], True, True, rd=[self.ones_f, self.hT[0], self.hT[1]], wr=[bk], mark=True)
            for gg in range(4):
                g = half * 4 + gg
                for cc in range(2):
                    ct = g * 2 + cc
                    self.stt(self.C2_t[:, ct, :], bk.ap[:, gg * 128:(gg + 1) * 128], self.lngb_t[:, 1, ct:ct + 1], self.bsb_t[:, g, :], ALU.mult, ALU.add,
                             rd=[bk, self.lngb, self.hT[2], self.hT[3]], wr=[self.C2])

    def l1_mixer(self):
        nc = self.nc
        self.rmsnorm(2, self.hnT)
        for ct in range(16):
            w = self.wload(self.oddu_d[ct], 2048)
            for hf in range(NH):
                bk = self.proj_fm(w, 16, self.hnT, hf)
                self.act(self.mixT[ct].ap[:, hf * 512:(hf + 1) * 512], bk.ap, AF.Gelu, rd=[bk], wr=[self.mixT[ct]])
        st = self.st_t
        for q2 in range(TP // 256):
            for cb in range(8):
                w = self.wload(self.oddv_d[cb], 4096)
                w3 = w.ap.rearrange("p (k c) -> p k c", k=16)
                for c2 in range(2):
                    ck = q2 * 2 + c2
                    bk = self.bank()
                    for kt in range(16):
                        self.mm(bk.ap[:, 0:256], self.hnT[kt].ap[:, ck * 128:(ck + 1) * 128], w3[:, kt, :], kt == 0, kt == 15, rd=[w, self.hnT[kt]], wr=[bk], mark=(kt == 15))
                    self.act(self.vg[c2].ap[:, cb * 256:(cb + 1) * 256], bk.ap[:, 0:256], AF.Gelu, rd=[bk], wr=[self.vg[c2]])
            for c2 in range(2):
                v = self.vg[c2]
                self.op("dve", nc.vector.tensor_reduce, dict(out=st[:, 0:1], in_=v.ap, axis=AX.X, op=ALU.add), rd=[v], wr=[self.st])
                self.op("dve", nc.vector.scalar_tensor_tensor, dict(out=self.junk.ap, in0=v.ap, scalar=1.0, in1=v.ap, op0=ALU.mult, op1=ALU.mult, accum_out=st[:, 1:2]),
                        rd=[v], wr=[self.junk, self.st])
                self.op("dve", nc.vector.tensor_scalar, dict(out=st[:, 2:3], in0=st[:, 0:1], scalar1=1.0 / 2048, scalar2=None, op0=ALU.mult), rd=[self.st], wr=[self.st], selfsync=True)
                self.tt(st[:, 3:4], st[:, 2:3], st[:, 2:3], ALU.mult, rd=[self.st], wr=[self.st])
                self.stt(st[:, 4:5], st[:, 1:2], 1.0 / 2048, st[:, 3:4], ALU.mult, ALU.subtract, rd=[self.st], wr=[self.st])
                self.act(st[:, 5:6], st[:, 4:5], AF.Sqrt, rd=[self.st, self.epsb], wr=[self.st], bias=self.eps_t[:, :])
                self.op("dve", nc.vector.reciprocal, dict(out=st[:, 6:7], in_=st[:, 5:6]), rd=[self.st], wr=[self.st])
                self.ts(v.ap, v.ap, st[:, 2:3], st[:, 6:7], ALU.subtract, ALU.mult, rd=[v, self.st], wr=[v])
            sl = slice(q2 * 256, (q2 + 1) * 256)
            for ct in range(16):
                bk = self.bank()
                for c2 in range(2):
                    self.mm(bk.ap[:, c2 * 128:(c2 + 1) * 128], self.vg[c2].ap[:, ct * 128:(ct + 1) * 128], self.wsb_t[:, ct // 2, :], True, True,
                            rd=[self.vg[c2], self.wsb], wr=[bk], mark=(c2 == 1))
                sg = self.sg[self.nsg % 2]
                self.nsg += 1
                for c2 in range(2):
                    cs = slice(c2 * 128, (c2 + 1) * 128)
                    self.stt(sg.ap[:, cs], bk.ap[:, cs], self.lngb_t[:, 0, ct:ct + 1], self.C2_t[:, ct, :], ALU.mult, ALU.add,
                             rd=[bk, self.lngb, self.C2], wr=[sg])
                self.tt(self.mixT[ct].ap[:, sl], self.mixT[ct].ap[:, sl], sg.ap[:, 0:256], ALU.mult, rd=[sg, self.mixT[ct]], wr=[self.mixT[ct]])
        self.outproj(self.wout1_d, self.mixT)

    def l0_setup(self):
        nc = self.nc
        f = F32
        self.hp_t = self.sb("hp", [128, 33], f)
        self.hp = Buf(self.hp_t[:])
        self.dma("sp", self.hp.ap, self.hp_d, wr=[self.hp])
        self.am0_t = self.sb("am0", [128, 128], f)
        self.am0 = Buf(self.am0_t[:])
        self.dma("sp", self.am0.ap, self.am0_d, wr=[self.am0])
        self.conv_t = self.sb("convw", [128, 24, 4], f)
        self.convw = Buf(self.conv_t[:])
        self.dma("sp", self.conv_t[:], self.conv_d, wr=[self.convw])
        self.am4_t = self.sb("am4", [128, 4, 2, 128], BF16)
        self.am4 = Buf(self.am4_t[:])
        self.am4f_t = self.sb("am4f", [128, 4, 2, 128], BF16)
        self.am4f = Buf(self.am4f_t[:])
        for i in range(4):
            self.cp("dve", self.am4_t[:, i, 0, :], self.cst_t[:, 576:704], rd=[self.cst], wr=[self.am4])
            self.cp("dve", self.am4_t[:, i, 1, :], self.cst_t[:, 704:832], rd=[self.cst], wr=[self.am4])
            self.cp("dve", self.am4f_t[:, i, 0, :], self.am0_t[:, :], rd=[self.am0], wr=[self.am4f])
            self.cp("dve", self.am4f_t[:, i, 1, :], self.cst_t[:, 704:832], rd=[self.cst], wr=[self.am4f])
        self.es_t = self.sb("es", [128, 16], f)
        self.esx = Buf(self.es_t[:])
        self.act(self.es_t[:], self.hp_t[:, 16:32], AF.Exp, rd=[self.hp], wr=[self.esx])
        nblk = TP // 128
        self.vtok_t = self.sb("vtok", [128, nblk + 1, 2, 128], BF16)
        self.vtok = Buf(self.vtok_t[:])
        self.op("dve", nc.vector.memset, dict(ap=self.vtok_t[:], constant=0.0), wr=[self.vtok])
        self.KT_t = self.sb("KT", [128, 2, 128 + TP], BF16)
        self.KT = [Buf(self.KT_t[:, i, :]) for i in range(2)]
        self.op("dve", nc.vector.memset, dict(ap=self.KT_t[:], constant=0.0), wr=self.KT)
        self.qTa = self.actT[0:8]
        self.P_t = self.sb("Pm", [128, 4, 2, 128], BF16)
        self.Pm = Buf(self.P_t[:])
        pass
        self.halo_t = self.sb("halo", [128, 24, 3], f)
        self.halo = Buf(self.halo_t[:])
        self.op("dve", nc.vector.memset, dict(ap=self.halo_t[:], constant=0.0), wr=[self.halo])
        self.S_t = self.sb("S", [128, 8, 128], f)
        self.S = [Buf(self.S_t[:, i, :]) for i in range(8)]
        self.op("dve", nc.vector.memset, dict(ap=self.S_t[:], constant=0.0), wr=self.S)
        self.nea_t = self.sb("nea", [128, 8], f)
        self.nea = Buf(self.nea_t[:])
        self.act(self.nea_t[:], self.hp_t[:, 0:8], AF.Exp, rd=[self.hp], wr=[self.nea])
        self.ts(self.nea_t[:], self.nea_t[:], -1.0, None, ALU.mult, None, rd=[self.nea], wr=[self.nea])
        names = ["U3n", "I3", "Sl3", "Su3", "mlo3", "mup3"]
        cols = [384, 512, 192, 448, 256, 320]
        self.c3 = {}
        for nm, c0 in zip(names, cols):
            self.c3[nm] = (self.cst_t[0:64, c0:c0 + 64].unsqueeze(1).to_broadcast([64, 8, 64]), self.cst)

        def mk(nm, shape, dt=f):
            t = self.sb(nm, shape, dt)
            return t, Buf(t[:])
        self.g_all = mk("g_all", [64, 8, 8])
        self.b_all = mk("b_all", [64, 8, 8])
        self.t88 = mk("t88", [64, 8, 8])
        self.cin = [mk("cin", [128, 515])] * 3
        self.cacc = mk("cacc", [128, 512])
        self.gqT = mk("gqT", [128, 512])
        self.gkT = mk("gkT", [128, 512])
        self.qs = self.gqT
        self.ks = self.gkT
        self.sqf = self.cacc
        self.gvT = mk("gvT", [128, 512])
        self.zs = mk("zs", [128, 512])
        self.g3 = mk("g3", [64, 8, 64])
        self.b3 = mk("b3", [64, 8, 64])
        self.rhs2 = mk("rhs2", [64, 8, 64])
        self.rhs3 = mk("rhs3", [64, 8, 64])
        self.ghc = mk("ghc", [64, 8])
        self.tA = self.g3
        self.tB = self.rhs3
        self.Dm = mk("Dm", [64, 8, 64])
        self.DmT = mk("DmT", [64, 8, 64])
        self.eg = mk("eg", [64, 8])
        self.egl = mk("egl", [64, 8])
        self.bg = mk("bg", [64, 8])
        self.glast = mk("glast", [128, 8])
        self.Mk = [mk("Mk%d" % i, [64, 8, 64]) for i in range(2)]
        self.Nk = [mk("Nk%d" % i, [64, 8, 64]) for i in range(2)]
        self.Rk = mk("Rk", [64, 8, 64])
        self.QKm = mk("QKm", [64, 8, 64])
        self.ktok = mk("ktok", [64, 8, 128])
        self.vtk = mk("vtk", [64, 8, 128])
        self.vb = self.vtk
        self.kbg = (self.vg_t[:, 1, :].bitcast(F32)[0:64, :].rearrange("p (a b) -> p a b", a=8), self.vg[1])
        self.kdec = self.ktok
        self.u_sb = (self.vg_t[:, 0, :].bitcast(F32)[0:64, :].rearrange("p (a b) -> p a b", a=8), self.vg[0])
        self.den = self.cacc[1]
        self.wT_sb = mk("wT_sb", [128, 512])
        self.qdT = mk("qdT", [128, 512])
        self.vnew = [mk("vnew%d" % i, [64, 128]) for i in range(2)]
        self.osb = (self.cin[0][0][:, 0:512], self.cin[0][1])

    def l0_mixer(self, prefix, first_own, lastp=True):
        kv = (not prefix) or lastp
        nc = self.nc
        self.rmsnorm(0, self.hnT)
        nblk = TP // 128
        for kvh in (range(2) if kv else ()):
            w = self.wload(self.win_d[8 + kvh], 2048)
            for hf in range(NH):
                bk = self.proj_fm(w, 16, self.hnT, hf)
                self.cp("act", self.KT[kvh].ap[:, 128 + hf * 512:128 + (hf + 1) * 512], bk.ap, rd=[bk], wr=[self.KT[kvh]])
        if kv:
            wv = self.wload(self.win_d[10], 2048)
            wv3 = wv.ap.rearrange("p (k c) -> p k c", k=16)
        for blk in (range(nblk) if kv else ()):
            bk = self.bank()
            for kt in range(16):
                self.mm(bk.ap[:, 0:128], self.hnT[kt].ap[:, blk * 128:(blk + 1) * 128], wv3[:, kt, :], kt == 0, kt == 15, rd=[wv, self.hnT[kt]], wr=[bk], mark=(kt == 15))
            self.cp("dve", self.vtok_t[:, 1 + blk, :, 64:128], bk.ap[:, 0:128].rearrange("p (a b) -> p a b", a=2), rd=[bk], wr=[self.vtok])
        if not prefix:
            for ct in range(8):
                w = self.wload(self.win_d[ct], 2048)
                for hf in range(NH):
                    bk = self.proj_fm(w, 16, self.hnT, hf)
                    self.cp("act", self.qTa[ct].ap[:, hf * 512:(hf + 1) * 512], bk.ap, rd=[bk], wr=[self.qTa[ct]])
            self.attention(first_own)
        for kvh in (range(2) if kv else ()):
            self.cp("dve", self.KT[kvh].ap[:, 0:128], self.KT[kvh].ap[:, TP:TP + 128], rd=[self.KT[kvh]], wr=[self.KT[kvh]])
        if kv:
            self.cp("dve", self.vtok_t[:, 0, :, :], self.vtok_t[:, nblk, :, :], rd=[self.vtok], wr=[self.vtok])
        wba = self.wload(self.wba_d, 256)
        for hf in range(NH):
            self.gdn_pre(wba, hf)
            for h in range(8):
                self.gdn_head(h, hf, prefix, skipq=(prefix and not lastp))
        if not prefix:
            self.outproj(self.wout0_d, self.mixT)

    def attention(self, first_own):
        nc = self.nc
        Pt = self.P_t
        for blk in range(TP // 128):
            am = self.am4f if (first_own and blk == 0) else self.am4
            ts_ = slice(blk * 128, (blk + 1) * 128)
            for kvh in range(2):
                for par in range(2):
                    pr = slice(par * 64, par * 64 + 64)
                    b0 = self.bank()
                    b1 = self.bank()
                    for i in range(4):
                        ct = kvh * 4 + i
                        bk = b0 if i < 2 else b1
                        for j in range(2):
                            kcols = slice(blk * 128 + j * 128, blk * 128 + (j + 1) * 128)
                            oc = ((i % 2) * 2 + j) * 128
                            self.mm(bk.ap[:, oc:oc + 128], self.KT[kvh].ap[pr, kcols], self.qTa[ct].ap[pr, ts_], True, True,
                                    rd=[self.KT[kvh], self.qTa[ct]], wr=[bk], mark=(i % 2 == 1 and j == 1))
                    self.act(Pt[:, 0:2, :, :], b0.ap.rearrange("p (a b c) -> p a b c", a=2, b=2), AF.Exp, rd=[b0], wr=[self.Pm], scale=0.125)
                    self.act(Pt[:, 2:4, :, :], b1.ap.rearrange("p (a b c) -> p a b c", a=2, b=2), AF.Exp, rd=[b1], wr=[self.Pm], scale=0.125)
                    self.tt(Pt[:], Pt[:], am.ap, ALU.mult, rd=[self.Pm, am], wr=[self.Pm])
                    bo = self.bank()
                    bs = self.bank()
                    M = 64 if par == 0 else 128
                    for j in range(2):
                        vl = self.vtok_t[:, blk + j, kvh, 64:128] if par == 0 else self.vtok_t[:, blk + j, kvh, :]
                        self.mm(bo.ap[0:M, :], vl, Pt[:, :, j, :], j == 0, j == 1, rd=[self.vtok, self.Pm], wr=[bo], mark=(j == 1))
                    for j in range(2):
                        self.mm(bs.ap[0:M, :], self.ones_t[:, 0:M], Pt[:, :, j, :], j == 0, j == 1, rd=[self.ones_bf, self.Pm], wr=[bs], mark=(j == 1))
                    h0 = kvh * 8 + par
                    dn = self.den.ap[pr, :]
                    for i in range(4):
                        hh = h0 + 2 * i
                        self.ts(dn[:, i * 128:(i + 1) * 128], bs.ap[pr, i * 128:(i + 1) * 128], self.es_t[pr, hh:hh + 1], None, ALU.add, None, rd=[bs, self.esx], wr=[self.den])
                    self.op("dve", nc.vector.reciprocal, dict(out=dn, in_=dn), rd=[self.den], wr=[self.den])
                    dst = self.mix_t[pr, kvh * 4:(kvh + 1) * 4, ts_]
                    self.tt(dst, bo.ap[pr, :].rearrange("p (a b) -> p a b", a=4), dn.rearrange("p (a b) -> p a b", a=4), ALU.mult,
                            rd=[bo, self.den], wr=self.mixT[kvh * 4:(kvh + 1) * 4])

    def gdn_pre(self, wba, hf):
        nc = self.nc
        bk = self.banks[7]
        w3 = wba.ap.rearrange("p (k c) -> p k c", k=16)
        for c in range(8):
            t0 = hf * 512 + c * 64
            for kt in range(16):
                self.mm(bk.ap[0:64, c * 16:(c + 1) * 16], self.hnT[kt].ap[:, t0:t0 + 64], w3[:, kt, :], kt == 0, kt == 15,
                        rd=[wba, self.hnT[kt]], wr=[bk], mark=(kt == 15 and c == 7))
        raw = bk.ap[0:64, 0:128].rearrange("p (c k) -> p c k", c=8)
        self.act(self.b_all[0][:], raw[:, :, 0:8], AF.Sigmoid, rd=[bk], wr=[self.b_all[1]])
        dtb = self.hp_t[0:64, 8:16].unsqueeze(1).to_broadcast([64, 8, 8])
        self.tt(self.t88[0][:], raw[:, :, 8:16], dtb, ALU.add, rd=[bk, self.hp], wr=[self.t88[1]])
        self.act(self.t88[0][:], self.t88[0][:], AF.Exp, rd=[self.t88[1]], wr=[self.t88[1]])
        self.act(self.t88[0][:], self.t88[0][:], AF.Ln, rd=[self.t88[1], self.ones_f], wr=[self.t88[1]], bias=self.onesf_t[0:64, 0:1])
        nea = self.nea_t[0:64, :].unsqueeze(1).to_broadcast([64, 8, 8])
        self.tt(self.g_all[0][:], self.t88[0][:], nea, ALU.mult, rd=[self.t88[1], self.nea], wr=[self.g_all[1]])

    def gdn_head(self, h, hf, prefix, skipq=False):
        nc = self.nc
        B = self.banks
        sl = slice(hf * 512, (hf + 1) * 512)
        f64 = self.onesf_t[0:64, 0:64]
        f128 = self.onesf_t[0:64, 0:128]
        U64 = self.cst_t[0:64, 128:192]
        L64 = self.cst_t[0:64, 192:256]
        C = lambda t: t[0]
        outs = [self.qs, self.ks, self.gvT]
        for j in ((1, 2) if skipq else (0, 1, 2)):
            w = self.wload(self.win_d[11 + h * 4 + j], 2048)
            bk = self.proj_fm(w, 16, self.hnT, hf)
            ct = j * 8 + h
            cin_t, cin_b = self.cin[j]
            self.cp("act", cin_t[:, 3:515], bk.ap, rd=[bk], wr=[cin_b])
            self.cp("dve", cin_t[:, 0:3], self.halo_t[:, ct, :], rd=[self.halo], wr=[cin_b])
            acc_t, acc_b = self.cacc
            cw = self.conv_t
            self.ts(acc_t[:], cin_t[:, 3:515], cw[:, ct, 3:4], None, ALU.mult, None, rd=[cin_b, self.convw], wr=[acc_b])
            for kk in (2, 1, 0):
                self.stt(acc_t[:], cin_t[:, kk:kk + 512], cw[:, ct, kk:kk + 1], acc_t[:], ALU.mult, ALU.add, rd=[cin_b, self.convw, acc_b], wr=[acc_b])
            self.cp("dve", self.halo_t[:, ct, :], cin_t[:, 512:515], rd=[cin_b], wr=[self.halo])
            self.act(outs[j][0][:], acc_t[:], AF.Silu, rd=[acc_b], wr=[outs[j][1]])
        if not prefix:
            w = self.wload(self.win_d[11 + h * 4 + 3], 2048)
            bk = self.proj_fm(w, 16, self.hnT, hf)
            self.act(self.zs[0][:], bk.ap, AF.Silu, rd=[bk], wr=[self.zs[1]])
        for (src, dst, scl) in ((self.qs, self.gqT, 128 ** -0.5), (self.ks, self.gkT, 1.0))[(1 if skipq else 0):]:
            self.act(self.sqf[0][:], src[0][:], AF.Square, rd=[src[1]], wr=[self.sqf[1]])
            bk = self.bank()
            self.mm(bk.ap, self.onesf_t[:, :], self.sqf[0][:], True, True, rd=[self.ones_f, self.sqf[1]], wr=[bk], mark=True)
            self.rstd_from_bank(bk, 512, 1.0)
            self.stt(dst[0][:], src[0][:], scl, self.rs.ap, ALU.mult, ALU.mult, rd=[src[1], self.rs], wr=[dst[1]])
        qT, kT, vT = self.gqT, self.gkT, self.gvT
        g_h = self.g_all[0][:, :, h]
        b_h = self.b_all[0][:, :, h]
        self.cp("dve", self.ghc[0][:], g_h, rd=[self.g_all[1]], wr=[self.ghc[1]])
        self.cp("dve", self.g3[0][:], g_h.unsqueeze(2).to_broadcast([64, 8, 64]), rd=[self.g_all[1]], wr=[self.g3[1]])
        self.cp("dve", self.b3[0][:], b_h.unsqueeze(2).to_broadcast([64, 8, 64]), rd=[self.b_all[1]], wr=[self.b3[1]])
        self.tt(self.rhs2[0][:], self.g3[0][:], self.c3["U3n"][0], ALU.mult, rd=[self.g3[1], self.c3["U3n"][1]], wr=[self.rhs2[1]])
        self.tt(self.rhs3[0][:], self.b3[0][:], self.c3["I3"][0], ALU.mult, rd=[self.b3[1], self.c3["I3"][1]], wr=[self.rhs3[1]])
        fl = lambda t: t[0][:].rearrange("p a b -> p (a b)")
        self.mm(B[0].ap[0:64, :], U64, fl(self.g3), True, False, rd=[self.cst, self.g3[1]], wr=[B[0]], mark=False)
        self.mm(B[0].ap[0:64, :], f64, fl(self.rhs2), False, True, rd=[self.ones_f, self.rhs2[1]], wr=[B[0]], mark=True)
        self.mm(B[1].ap[0:64, :], f64, fl(self.rhs3), True, True, rd=[self.ones_f, self.rhs3[1]], wr=[B[1]], mark=True)
        self.mm(B[2].ap[0:64, 0:8], U64, self.ghc[0][:], True, True, rd=[self.cst, self.ghc[1]], wr=[B[2]], mark=False)
        self.mm(B[2].ap[0:64, 8:16], L64, self.ghc[0][:], True, True, rd=[self.cst, self.ghc[1]], wr=[B[2]], mark=False)
        self.mm(B[2].ap[0:128, 16:24], f128, self.ghc[0][:], True, True, rd=[self.ones_f, self.ghc[1]], wr=[B[2]], mark=True)
        if not prefix:
            self.mm(B[3].ap[:, :], f128, fl(self.rhs2), True, True, rd=[self.ones_f, self.rhs2[1]], wr=[B[3]], mark=True)
        for c in range(8):
            cs = slice(c * 64, (c + 1) * 64)
            self.mm(B[4].ap[0:64, cs], kT[0][:, cs], kT[0][:, cs], True, True, rd=[kT[1]], wr=[B[4]], mark=(c == 7))
        if not prefix:
            for c in range(8):
                cs = slice(c * 64, (c + 1) * 64)
                self.mm(B[5].ap[0:64, cs], kT[0][:, cs], qT[0][:, cs], True, True, rd=[kT[1], qT[1]], wr=[B[5]], mark=(c == 7))
        v3 = lambda ap: ap.rearrange("p (a b) -> p a b", a=8)
        D3 = v3(B[0].ap[0:64, :])
        self.tt(self.tA[0][:], D3, self.c3["mlo3"][0], ALU.add, rd=[B[0], self.c3["mlo3"][1]], wr=[self.tA[1]])
        self.act(self.Dm[0][:], self.tA[0][:], AF.Exp, rd=[self.tA[1]], wr=[self.Dm[1]])
        self.stt(self.tB[0][:], D3, -1.0, self.c3["mup3"][0], ALU.mult, ALU.add, rd=[B[0], self.c3["mup3"][1]], wr=[self.tB[1]])
        self.act(self.DmT[0][:], self.tB[0][:], AF.Exp, rd=[self.tB[1]], wr=[self.DmT[1]])
        self.act(self.eg[0][:], B[2].ap[0:64, 0:8], AF.Exp, rd=[B[2]], wr=[self.eg[1]])
        self.act(self.egl[0][:], B[2].ap[0:64, 8:16], AF.Exp, rd=[B[2]], wr=[self.egl[1]])
        self.act(self.glast[0][:], B[2].ap[0:128, 16:24], AF.Exp, rd=[B[2]], wr=[self.glast[1]])
        if not prefix:
            self.act(self.qdT[0][:], B[3].ap[:, :], AF.Exp, rd=[B[3]], wr=[self.qdT[1]], scale=-1.0)
        KK3 = v3(B[4].ap[0:64, :])
        self.stt(self.tA[0][:], self.Dm[0][:], -1.0, self.c3["Sl3"][0], ALU.mult, ALU.mult, rd=[self.Dm[1], self.c3["Sl3"][1]], wr=[self.tA[1]])
        self.tt(self.tA[0][:], self.tA[0][:], self.b3[0][:], ALU.mult, rd=[self.tA[1], self.b3[1]], wr=[self.tA[1]])
        self.tt(self.Nk[0][0][:], KK3, self.tA[0][:], ALU.mult, rd=[B[4], self.tA[1]], wr=[self.Nk[0][1]])
        self.stt(self.tB[0][:], self.DmT[0][:], -1.0, self.c3["Su3"][0], ALU.mult, ALU.mult, rd=[self.DmT[1], self.c3["Su3"][1]], wr=[self.tB[1]])
        self.tt(self.tB[0][:], self.tB[0][:], v3(B[1].ap[0:64, :]), ALU.mult, rd=[self.tB[1], B[1]], wr=[self.tB[1]])
        self.tt(self.Mk[0][0][:], KK3, self.tB[0][:], ALU.mult, rd=[B[4], self.tB[1]], wr=[self.Mk[0][1]])
        if not prefix:
            self.tt(self.QKm[0][:], v3(B[5].ap[0:64, :]), self.DmT[0][:], ALU.mult, rd=[B[5], self.DmT[1]], wr=[self.QKm[1]])
        self.tt(self.Rk[0][:], self.Mk[0][0][:], self.c3["I3"][0], ALU.add, rd=[self.Mk[0][1], self.c3["I3"][1]], wr=[self.Rk[1]])
        for (src, dst, b0) in ((kT, self.ktok, 6), (vT, self.vtk, 0)):
            for c in range(8):
                bk = B[b0 + c // 4]
                self.op("pe", nc.tensor.transpose, dict(out=bk.ap[0:64, (c % 4) * 128:(c % 4 + 1) * 128], in_=src[0][:, c * 64:(c + 1) * 64], identity=self.ident),
                        rd=[src[1], self.cst], wr=[bk], mark=(c % 4 == 3))
            for hh in range(2):
                self.cp("act", dst[0][:, hh * 4:(hh + 1) * 4, :], B[b0 + hh].ap[0:64, :].rearrange("p (a b) -> p a b", a=4), rd=[B[b0 + hh]], wr=[dst[1]])
        cur = 0
        for lvl in range(1, 6):
            nxt = 1 - cur
            Mc, Nc = self.Mk[cur], self.Nk[cur]
            Mn, Nn = self.Mk[nxt], self.Nk[nxt]
            if lvl < 5:
                for c in range(8):
                    cs = slice(c * 64, (c + 1) * 64)
                    self.mm(B[2].ap[0:64, cs], Nc[0][:, c, :], Mc[0][:, c, :], True, True, rd=[Nc[1], Mc[1]], wr=[B[2]], mark=(c == 7))
            for c in range(8):
                cs = slice(c * 64, (c + 1) * 64)
                self.mm(B[3].ap[0:64, cs], Mc[0][:, c, :], Nc[0][:, c, :], True, True, rd=[Nc[1], Mc[1]], wr=[B[3]], mark=(c == 7))
            if lvl < 5:
                self.cp("act", Mn[0][:], v3(B[2].ap[0:64, :]), rd=[B[2]], wr=[Mn[1]])
            self.cp("dve", Nn[0][:], v3(B[3].ap[0:64, :]), rd=[B[3]], wr=[Nn[1]])
            for c in range(8):
                cs = slice(c * 64, (c + 1) * 64)
                self.mm(B[4].ap[0:64, cs], Nn[0][:, c, :], self.Rk[0][:, c, :], True, True, rd=[Nn[1], self.Rk[1]], wr=[B[4]], mark=(c == 7))
            self.tt(self.Rk[0][:], self.Rk[0][:], v3(B[4].ap[0:64, :]), ALU.add, rd=[B[4], self.Rk[1]], wr=[self.Rk[1]])
            cur = nxt
        self.tt(self.bg[0][:], b_h, self.eg[0][:], ALU.mult, rd=[self.b_all[1], self.eg[1]], wr=[self.bg[1]])
        self.tt(self.vb[0][:], self.vtk[0][:], b_h.unsqueeze(2).to_broadcast([64, 8, 128]), ALU.mult, rd=[self.vtk[1], self.b_all[1]], wr=[self.vb[1]])
        self.tt(self.kbg[0][:], self.ktok[0][:], self.bg[0][:].unsqueeze(2).to_broadcast([64, 8, 128]), ALU.mult, rd=[self.ktok[1], self.bg[1]], wr=[self.kbg[1]])
        self.tt(self.kdec[0][:], self.ktok[0][:], self.egl[0][:].unsqueeze(2).to_broadcast([64, 8, 128]), ALU.mult, rd=[self.ktok[1], self.egl[1]], wr=[self.kdec[1]])
        for c in range(8):
            bk = B[5 + c // 4]
            self.mm(bk.ap[0:64, (c % 4) * 128:(c % 4 + 1) * 128], self.Rk[0][:, c, :], self.vb[0][:, c, :], True, True, rd=[self.Rk[1], self.vb[1]], wr=[bk], mark=(c % 4 == 3))
        for hh in range(2):
            self.cp("act", self.u_sb[0][:, hh * 4:(hh + 1) * 4, :], B[5 + hh].ap[0:64, :].rearrange("p (a b) -> p a b", a=4), rd=[B[5 + hh]], wr=[self.u_sb[1]])
        for c in range(8):
            cs = slice(c * 64, (c + 1) * 64)
            self.mm(B[7].ap[:, cs], self.kbg[0][:, c, :], self.Rk[0][:, c, :], True, True, rd=[self.kbg[1], self.Rk[1]], wr=[B[7]], mark=(c == 7))
        self.cp("dve", self.wT_sb[0][:], B[7].ap[:, :], rd=[B[7]], wr=[self.wT_sb[1]])
        if not prefix:
            self.tt(self.qdT[0][:], qT[0][:], self.qdT[0][:], ALU.mult, rd=[qT[1], self.qdT[1]], wr=[self.qdT[1]])
        S = self.S[h]
        for c in range(8):
            cs = slice(c * 64, (c + 1) * 64)
            vb_ = B[1 + c % 2]
            self.mm(vb_.ap[0:64, 0:128], self.wT_sb[0][:, cs], S.ap, True, True, rd=[self.wT_sb[1], S], wr=[vb_], mark=True)
            vn = self.vnew[c % 2]
            self.tt(vn[0][:], self.u_sb[0][:, c, :], vb_.ap[0:64, 0:128], ALU.subtract, rd=[self.u_sb[1], vb_], wr=[vn[1]])
            if not prefix:
                self.mm(B[0].ap[:, cs], S.ap, self.qdT[0][:, cs], True, False, rd=[S, self.qdT[1]], wr=[B[0]], mark=False)
                self.mm(B[0].ap[:, cs], vn[0][:], self.QKm[0][:, c, :], False, True, rd=[vn[1], self.QKm[1]], wr=[B[0]], mark=False)
            sb_ = B[3 + c % 2]
            self.mm(sb_.ap[:, 0:128], self.kdec[0][:, c, :], vn[0][:], True, True, rd=[self.kdec[1], vn[1]], wr=[sb_], mark=True)
            self.stt(S.ap, S.ap, self.glast[0][:, c:c + 1], sb_.ap[:, 0:128], ALU.mult, ALU.add, rd=[S, self.glast[1], sb_], wr=[S])
        if not prefix:
            self.cp("act", self.osb[0][:], B[0].ap[:, :], rd=[B[0]], wr=[self.osb[1]])
            self.act(self.sqf[0][:], self.osb[0][:], AF.Square, rd=[self.osb[1]], wr=[self.sqf[1]])
            bk = B[5]
            self.mm(bk.ap, self.onesf_t[:, :], self.sqf[0][:], True, True, rd=[self.ones_f, self.sqf[1]], wr=[bk], mark=True)
            self.rstd_from_bank(bk, 512, 1.0 / 128)
            self.stt(self.osb[0][:], self.osb[0][:], self.hp_t[:, 32:33], self.rs.ap, ALU.mult, ALU.mult, rd=[self.osb[1], self.hp, self.rs], wr=[self.osb[1]])
            self.tt(self.mixT[8 + h].ap[:, sl], self.osb[0][:], self.zs[0][:], ALU.mult, rd=[self.osb[1], self.zs[1]], wr=[self.mixT[8 + h]])


def blockify(Wc):
    K, C = Wc.shape
    return np.ascontiguousarray(Wc.reshape(K // 128, 128, C).transpose(1, 0, 2)).reshape(128, (K // 128) * C)


def col16(v):
    return np.ascontiguousarray(v.reshape(16, 128).T)


def make_consts():
    c = np.zeros((128, NCST), np.float32)
    c[:, 0:128] = np.eye(128, dtype=np.float32)
    i = np.arange(64)
    c[0:64, 128:192] = (i[:, None] <= i[None, :])
    c[0:64, 192:256] = (i[:, None] > i[None, :])
    c[0:64, 256:320] = np.where(i[:, None] >= i[None, :], 0.0, -1e4)
    c[0:64, 320:384] = np.where(i[None, :] >= i[:, None], 0.0, -1e4)
    c[0:64, 384:448] = -(i[:, None] <= i[None, :]).astype(np.float32)
    c[0:64, 448:512] = (i[None, :] > i[:, None])
    c[0:64, 512:576] = np.eye(64)
    k = np.arange(128)
    c[:, 576:704] = (k[:, None] > k[None, :])
    c[:, 704:832] = (k[:, None] <= k[None, :])
    return c


STAGES = dict(l0=True, ffn0=True, l1=True, ffn1=True, fnorm=True)
_NC_CACHE = {}


def prep_shared(inp):
    f = np.float32
    sh = {}
    sh["cst"] = make_consts()
    norms = np.zeros((128, 6, 16), f)
    norms[:, 0] = col16(inp["even_norm"][0])
    norms[:, 1] = col16(inp["ffn_norm"][0])
    norms[:, 2] = col16(inp["odd_norm"][0])
    norms[:, 3] = col16(inp["ffn_norm"][1])
    norms[:, 4] = col16(inp["final_norm"])
    sh["norms"] = norms
    wi = inp["even_w_in"][0]
    blocks = []
    for ct in range(8):
        blocks.append(blockify(wi[:, ct * 128:(ct + 1) * 128]))
    for kvh in range(2):
        kc = wi[:, 1024 + kvh * 64:1024 + (kvh + 1) * 64]
        blocks.append(blockify(np.concatenate([kc, kc], axis=1)))
    blocks.append(blockify(wi[:, 1152:1280]))
    for h in range(8):
        for base in (1280, 2304, 3328, 4352):
            blocks.append(blockify(wi[:, base + h * 128:base + (h + 1) * 128]))
    sh["win"] = np.stack(blocks)
    sh["wba"] = blockify(wi[:, 5376:5392])
    cv = inp["even_conv"][0]
    sh["conv"] = np.ascontiguousarray(cv.T.reshape(24, 128, 4).transpose(1, 0, 2))
    hp = np.zeros((128, 33), f)
    hp[:, 0:8] = inp["even_a_log"][0][None, :]
    hp[:, 8:16] = inp["even_dt_bias"][0][None, :]
    hp[:, 16:32] = inp["even_sinks"][0][None, :]
    hp[:, 32] = inp["even_onorm"][0]
    sh["hp"] = hp
    wo = inp["even_w_out"][0]
    sh["wout0"] = np.stack([blockify(wo[:, d * 128:(d + 1) * 128]) for d in range(16)])
    ow = inp["odd_w_in"][0]
    sh["oddu"] = np.stack([blockify(ow[:, c * 128:(c + 1) * 128]) for c in range(16)])
    sh["oddv"] = np.stack([blockify(ow[:, 2048 + c * 256:2048 + (c + 1) * 256]) for c in range(8)])
    lngb = np.zeros((128, 2, 16), f)
    lngb[:, 0] = col16(inp["odd_ln_g"][0])
    lngb[:, 1] = col16(inp["odd_ln_b"][0])
    sh["lngb"] = lngb
    sh["wsT"] = np.ascontiguousarray(inp["odd_w_s"][0].transpose(2, 0, 1))
    sh["bsb"] = np.ascontiguousarray(np.broadcast_to(inp["odd_b_s"][0][None], (128, 8, 128)))
    wo1 = inp["odd_w_out"][0]
    sh["wout1"] = np.stack([blockify(wo1[:, d * 128:(d + 1) * 128]) for d in range(16)])
    sh["wg"] = np.stack([np.stack([blockify(inp["ffn_w_gate"][l][:, c * 128:(c + 1) * 128]) for c in range(44)]) for l in range(2)])
    sh["wu"] = np.stack([np.stack([blockify(inp["ffn_w_up"][l][:, c * 128:(c + 1) * 128]) for c in range(44)]) for l in range(2)])
    sh["wd"] = np.stack([np.stack([blockify(inp["ffn_w_down"][l][:, c * 128:(c + 1) * 128]) for c in range(16)]) for l in range(2)])
    return sh


def kernel(**inp):
    inp = {k: np.asarray(v) for k, v in inp.items()}
    key = tuple(sorted(STAGES.items()))
    if key not in _NC_CACHE:
        _NC_CACHE[key] = KB(dict(STAGES)).build()
    nc = _NC_CACHE[key]
    sh = prep_shared(inp)
    x = inp["x"]
    k = np.arange(128)
    su = (k[:, None] > k[None, :]).astype(np.float32)
    in_maps = []
    for c in range(8):
        b, half = c // 2, c % 2
        m = dict(sh)
        m["xo"] = np.ascontiguousarray(x[b, half * 2048:(half + 1) * 2048])
        m["xp"] = np.ascontiguousarray(x[b, 0:2048]) if half == 1 else np.zeros((2048, D), np.float32)
        m["am0"] = su if half == 1 else np.zeros((128, 128), np.float32)
        in_maps.append(m)
    import os
    res = run_bass_kernel_spmd(nc, in_maps, core_ids=list(range(8)))
    out = np.zeros((4, 4096, D), np.float32)
    for c in range(8):
        b, half = c // 2, c % 2
        out[b, half * 2048:(half + 1) * 2048] = res.results[c]["out"]
    return out
```
